# Optimizing a Trainium2 kernel written in Bass

```python
import jax, jax.numpy as jnp
from jax import lax
import numpy as np

D_MODEL = 2048
BATCH = 4
SEQ = 2048
DEPTH = 4

N_META = 16
SB_HEADS = 8
SB_HEAD_DIM = D_MODEL // 16
SB_WIDTH = SB_HEADS * SB_HEAD_DIM
POOL_WINDOWS = (2, 4, 8, 16)
POOL_GROUPS = len(POOL_WINDOWS)
POOL_WIDTH = D_MODEL // 2
POOL_GROUP_DIM = POOL_WIDTH // POOL_GROUPS
Q_BLOCK = 128
RMS_EPS = 1e-6
IN_SIZES = (SB_WIDTH, SB_WIDTH, SB_WIDTH, SB_WIDTH, POOL_WIDTH, POOL_WIDTH, D_MODEL, D_MODEL)
IN_COLS = sum(IN_SIZES)

kernel_name = "hybrid_stickbreak_pool_gated_trunk"


def rmsnorm(x, gain):
    xf = x.astype(jnp.float32)
    y = xf * lax.rsqrt(jnp.mean(xf * xf, axis=-1, keepdims=True) + RMS_EPS)
    return (y * gain.astype(jnp.float32)).astype(x.dtype)


def stick_breaking_block(q_blk, k_pre, v_pre, q0):
    nq = q_blk.shape[1]
    nk = k_pre.shape[1]
    z = jnp.einsum('bqhd,bkhd->bhqk', q_blk.astype(jnp.float32), k_pre.astype(jnp.float32)) * (SB_HEAD_DIM ** -0.5)
    qpos = q0 + jnp.arange(nq)
    kpos = jnp.arange(nk)
    causal = kpos[None, :] < qpos[:, None]
    log_1m_beta = jnp.where(causal, jax.nn.log_sigmoid(-z), 0.0)
    between = lax.cumsum(log_1m_beta, axis=3, reverse=True) - log_1m_beta
    a = jnp.where(causal, jnp.exp(jax.nn.log_sigmoid(z) + between), 0.0)
    o = jnp.einsum('bhqk,bkhd->bqhd', a, v_pre.astype(jnp.float32))
    return o


def stick_breaking_attention(q, k, v):
    L = q.shape[1]
    bounds = [(0, N_META)] + [(s, min(s + Q_BLOCK, L)) for s in range(N_META, L, Q_BLOCK)]
    outs = [stick_breaking_block(q[:, s:e], k[:, :e], v[:, :e], s) for (s, e) in bounds]
    return jnp.concatenate(outs, axis=1)


def multiscale_pool(u):
    L = u.shape[1]
    wmax = max(POOL_WINDOWS)
    c = jnp.cumsum(jnp.pad(u, ((0, 0), (wmax, 0), (0, 0))), axis=1)
    steps = jnp.arange(1, L + 1, dtype=jnp.float32)[None, :, None]
    outs = []
    for g, w in enumerate(POOL_WINDOWS):
        sl = slice(g * POOL_GROUP_DIM, (g + 1) * POOL_GROUP_DIM)
        win_sum = c[:, wmax:, sl] - c[:, wmax - w:wmax - w + L, sl]
        cnt = jnp.minimum(steps, float(w))
        outs.append(win_sum / cnt - u[:, :, sl])
    return jnp.stack(outs, axis=2)


def hybrid_layer(x, gain, w_in, pool_w, pool_scale, w_attn_up, w_pool_up, w_out):
    B, L, _ = x.shape
    h = rmsnorm(x, gain)
    zin = h @ w_in.astype(h.dtype)
    cuts = np.cumsum(IN_SIZES)[:-1].tolist()
    q, k, v, g_attn, u, g_pool, m_attn, m_pool = jnp.split(zin, cuts, axis=-1)
    q = q.reshape(B, L, SB_HEADS, SB_HEAD_DIM)
    k = k.reshape(B, L, SB_HEADS, SB_HEAD_DIM)
    v = v.reshape(B, L, SB_HEADS, SB_HEAD_DIM)
    o_attn = stick_breaking_attention(q, k, v).reshape(B, L, SB_WIDTH)
    o_attn = (o_attn * jax.nn.silu(g_attn.astype(jnp.float32))).astype(x.dtype)
    y_attn = o_attn @ w_attn_up.astype(x.dtype)
    pooled = multiscale_pool(u.astype(jnp.float32))
    mixed = jnp.einsum('blgc,gcd->blgd', pooled, pool_w.astype(jnp.float32)).reshape(B, L, POOL_WIDTH)
    o_pool = (mixed * pool_scale.astype(jnp.float32) * jax.nn.silu(g_pool.astype(jnp.float32))).astype(x.dtype)
    y_pool = o_pool @ w_pool_up.astype(x.dtype)
    merged = jax.nn.sigmoid(m_attn) * y_attn + jax.nn.sigmoid(m_pool) * y_pool
    return x + merged @ w_out.astype(x.dtype)


def setup_inputs(seed: int = 0) -> dict:
    key = jax.random.key(seed)
    ks = jax.random.split(key, 10)
    f32 = jnp.float32
    x = jax.random.normal(ks[0], (BATCH, SEQ, D_MODEL), f32)
    meta_tokens = jax.random.normal(ks[1], (N_META, D_MODEL), f32)
    norm_gain = 1.0 + 0.02 * jax.random.normal(ks[2], (DEPTH, D_MODEL), f32)
    w_in = jax.random.normal(ks[3], (DEPTH, D_MODEL, IN_COLS), f32) * D_MODEL ** -0.5
    pool_w = jax.random.normal(ks[4], (DEPTH, POOL_GROUPS, POOL_GROUP_DIM, POOL_GROUP_DIM), f32) * POOL_GROUP_DIM ** -0.5
    pool_scale = 1.0 + 0.1 * jax.random.normal(ks[5], (DEPTH, POOL_WIDTH), f32)
    w_attn_up = jax.random.normal(ks[6], (DEPTH, SB_WIDTH, D_MODEL), f32) * SB_WIDTH ** -0.5
    w_pool_up = jax.random.normal(ks[7], (DEPTH, POOL_WIDTH, D_MODEL), f32) * POOL_WIDTH ** -0.5
    w_out = jax.random.normal(ks[8], (DEPTH, D_MODEL, D_MODEL), f32) * D_MODEL ** -0.5
    final_gain = 1.0 + 0.02 * jax.random.normal(ks[9], (D_MODEL,), f32)
    return {"x": x, "meta_tokens": meta_tokens, "norm_gain": norm_gain, "w_in": w_in,
            "pool_w": pool_w, "pool_scale": pool_scale, "w_attn_up": w_attn_up,
            "w_pool_up": w_pool_up, "w_out": w_out, "final_gain": final_gain}


def reference(x, meta_tokens, norm_gain, w_in, pool_w, pool_scale, w_attn_up, w_pool_up, w_out, final_gain):
    B = x.shape[0]
    meta = jnp.broadcast_to(meta_tokens.astype(x.dtype)[None], (B, N_META, x.shape[2]))
    hs = jnp.concatenate([meta, x], axis=1)
    for layer in range(DEPTH):
        hs = hybrid_layer(hs, norm_gain[layer], w_in[layer], pool_w[layer], pool_scale[layer],
                          w_attn_up[layer], w_pool_up[layer], w_out[layer])
    return rmsnorm(hs, final_gain)[:, N_META:]
```

```python
import numpy as np
from contextlib import ExitStack
import concourse.bass as bass
import concourse.mybir as mybir
from concourse.bass_utils import run_bass_kernel_spmd

F32 = mybir.dt.float32
BF16 = mybir.dt.bfloat16
F32R = mybir.dt.float32r
AF = mybir.ActivationFunctionType
ALU = mybir.AluOpType

D = 2048
KC = 16
NL = 4
H = 8
DH = 128
PRE = 16
RT = 1024
T = PRE + RT
NB = RT // 128
GW = 256
PCH = [(0, 347), (347, 694), (694, 1040)]
NCH = [(0, 16)] + [(16 + 256 * i, 16 + 256 * (i + 1)) for i in range(4)]
POOL_W = (2, 4, 8, 16)
EPS = 1e-6
NEG = -30000.0
N_CORES = 8
NWBUF = 3
DEBUG_STOP = None
DEBUG_TINY = False


class DSem:
    def __init__(self, h):
        self.h = h
        self.cnt = 0


class Prog:
    ENG = ("pe", "act", "dve", "pool", "sp")

    def __init__(self, nc, st):
        self.nc = nc
        self.st = st
        self.q = {k: [] for k in self.ENG}
        self.sem = {k: st.enter_context(nc.semaphore("sem_" + k)) for k in self.ENG}
        self.cnt = {k: 0 for k in self.ENG}

    def dsem(self, name):
        return DSem(self.st.enter_context(self.nc.semaphore(name)))

    def I(self, eng, name, waits, *args, sig=True, **kw):
        ev = None
        if sig:
            self.cnt[eng] += 1
            ev = (self.sem[eng], self.cnt[eng])
        self.q[eng].append((name, args, kw, [w for w in waits if w is not None], ev, 1))
        return ev

    def dma(self, eng, out, in_, sem, waits=(), **kw):
        sem.cnt += 16
        ev = (sem.h, sem.cnt)
        kw = dict(kw)
        kw.update(out=out, in_=in_)
        self.q[eng].append(("dma_start", (), kw, [w for w in waits if w is not None], ev, 16))
        return ev

    def cc(self, sem, waits, **kw):
        sem.cnt += 1
        ev = (sem.h, sem.cnt)
        self.q["pool"].append(("collective_compute", (), kw, [w for w in waits if w is not None], ev, 1))
        return ev

    def last(self, eng):
        return (self.sem[eng], self.cnt[eng]) if self.cnt[eng] > 0 else None

    def barrier(self, engs=("pe", "act", "dve", "sp"), extra=()):
        evs = [self.last(o) for o in engs] + list(extra)
        for e in engs:
            self.q[e].append((None, (), {}, [w for w in evs if w is not None and w[0] is not self.sem[e]], None, 0))

    def replay(self, eng, e):
        waited = {}
        for name, args, kw, waits, ev, inc in self.q[eng]:
            for (h, v) in waits:
                k = h.num
                if waited.get(k, 0) >= v:
                    continue
                e.wait_ge(h, v)
                waited[k] = v
            if name is None:
                continue
            ins = getattr(e, name)(*args, **kw)
            if ev is not None:
                ins.then_inc(ev[0], inc)


def build(depth=NL):
    nc = bass.Bass("TRN2", target_bir_lowering=False)

    def din(name, shape, dt=F32):
        return nc.dram_tensor(name, shape, dt, kind="ExternalInput")

    xT0 = din("xT0", [16, 128, T])
    NG = 1 if DEBUG_TINY else 40
    w_in_l = [din(f"w_in_r{l}", [NG, 128, 16 * GW]) for l in range(depth)]
    NG8 = 1 if DEBUG_TINY else depth * 8
    w_au_r = din("w_au_r", [NG8, 128, 8 * GW])
    w_pu_r = din("w_pu_r", [NG8, 128, 8 * GW])
    w_out_r = din("w_out_r", [NG8, 128, 16 * GW])
    pool_w_r = din("pool_w_r", [depth, 128, 4 * 2 * 256])
    gains_d = din("gains", [128, NL * 16 + 16])
    pscale_d = din("pscale", [128, NL * 8])
    flags_d = din("flags", [128, 8])
    invcnt_d = din("invcnt", [128, 4 * 16])
    ones_d = din("ones_f32", [128, 128])
    negtri_d = din("negtri", [128, 128])
    negones_d = din("negones", [128, 128])
    diag_d = din("diagmask", [128, 128])
    outT = nc.dram_tensor("outT", [16, 128, RT], F32, kind="ExternalOutput")

    xres = nc.dram_tensor("xres", [16, 128, T], F32)
    sendK = [[nc.dram_tensor(f"sendK{l}_{a}", [4 * 128, T], BF16) for a in range(2)] for l in range(depth)]
    recvK = [[nc.dram_tensor(f"recvK{l}_{a}", [2 * 4 * 128, T], BF16) for a in range(2)] for l in range(depth)]
    sendV = [[nc.dram_tensor(f"sendV{l}_{a}", [9 * 128, 512], BF16) for a in range(2)] for l in range(depth)]
    recvV = [[nc.dram_tensor(f"recvV{l}_{a}", [2 * 9 * 128, 512], BF16) for a in range(2)] for l in range(depth)]
    sendU = [nc.dram_tensor(f"sendU{l}", [128, 128], F32) for l in range(depth)]
    recvU = [nc.dram_tensor(f"recvU{l}", [256, 128], F32) for l in range(depth)]
    gate_d = [[nc.dram_tensor(f"gate{l}_{k}", [16, 128, T], F32) for k in range(2)] for l in range(depth)]
    GROUPS = [[0, 1], [2, 3], [4, 5], [6, 7]]

    with ExitStack() as st:
        def sb(name, shape, dt):
            return st.enter_context(nc.sbuf_tensor(name, shape, dt))

        P = Prog(nc, st)
        I = P.I

        hT = sb("hT", [128, KC, T], BF16)
        RA = sb("RA", [128, 8768], F32)
        xst_pre = sb("xst_pre", [128, KC, 16], F32)
        RB = sb("RB", [128, 12928], F32)
        o_pool = sb("o_pool", [128, 8, T], BF16)
        wbufs = [sb(f"wbuf{i}", [128, 16 * GW], BF16) for i in range(NWBUF)]
        pool_w_sb = sb("pool_w_sb", [128, 4 * 2 * 256], BF16)
        RC = sb("RC", [128, 6240], F32)
        RD = sb("RD", [128, 4384], F32)
        gains = sb("gains_sb", [128, NL * 16 + 16], F32)
        pscale = sb("pscale_sb", [128, NL * 8], F32)
        flags = sb("flags_sb", [128, 8], F32)
        invcnt = sb("invcnt_sb", [128, 4, 16], F32)
        ones_f = sb("ones_sb", [128, 128], F32)
        negtri = sb("negtri_sb", [128, 128], BF16)
        negones = sb("negones_sb", [128, 128], BF16)
        diagm = sb("diag_sb", [128, 128], BF16)

        def view(region, boff, dt, shape):
            esz = 4 if dt == F32 else 2
            n = int(np.prod(shape[1:])) * esz
            ap = region[:, boff // 4:(boff + n) // 4]
            if dt != F32:
                ap = ap.bitcast(dt)
            if len(shape) == 3:
                ap = ap.rearrange("p (a b) -> p a b", a=shape[1])
            return ap

        xst = [view(RA, i * 16384, F32, [128, KC, 256]) for i in range(2)]
        KT_own = view(RA, 0, BF16, [128, H, T])
        V_own = view(RA, 16640, BF16, [128, 9, 1024])
        merged = view(RA, 0, BF16, [128, KC, T])
        UBUF = view(RB, 0, F32, [128, 8, 16 + T])
        o_attn = view(RB, 0, BF16, [128, H, T])
        ASC = 16640
        Ebuf = view(RB, ASC, F32, [128, 512])
        ARGb = [view(RB, ASC + 2048 + i * 2048, F32, [128, 512]) for i in range(3)]
        Rb = [view(RB, ASC + 8192 + i * 2048, F32, [128, 512]) for i in range(2)]
        SPb = [view(RB, ASC + 12288 + i * 1024, BF16, [128, 512]) for i in range(3)]
        Ab = [view(RB, ASC + 15360 + i * 1024, BF16, [128, 512]) for i in range(3)]
        QT2 = [view(RB, ASC + 18432 + i * 8320, BF16, [128, 2, T]) for i in range(2)]
        SGA2 = [view(RB, ASC + 18432 + 4160 + i * 8320, BF16, [128, 2, T]) for i in range(2)]
        PA = view(RC, 0, F32, [128, 16 + T])
        PBb = view(RC, 4224, F32, [128, 16 + T])
        pooledbf = view(RC, 8448, BF16, [128, 2, T])
        sgp2 = [view(RC, 12608, BF16, [128, 2, T]), view(RC, 17344, BF16, [128, 2, T])]
        uhalo = view(RC, 16768, F32, [128, 8, 16])
        tmp16 = view(RC, 17280, F32, [128, 16])
        SGA = [view(RC, i * 4160, F32, [128, T]) for i in range(2)]
        SGP = [view(RC, 8320 + i * 4160, F32, [128, T]) for i in range(2)]
        T1 = [view(RC, 16640 + i * 4160, F32, [128, T]) for i in range(2)]
        GST = [view(RC, i * 4160, F32, [128, T]) for i in range(2)]
        yst = view(RC, 0, F32, [128, 2, 512])
        KT_par = view(RD, 0, BF16, [128, 4, T])
        V_par = view(RD, 8320, BF16, [128, 4, 9 * 128])
        xc = view(RD, 0, F32, [128, 2, T])
        sq = view(RD, 0, F32, [128, 2, 512])
        sqb = view(RD, 0, BF16, [128, 2, 512])
        rs_tmp = [view(RD, 4096 + i * 1024, F32, [128, 256]) for i in range(2)]
        rstd = [view(RD, 6144 + i * 1024, F32, [128, 256]) for i in range(2)]

        ps_all = st.enter_context(nc.psum_tensor("ps_all", [128, 4096], F32))
        banks = [ps_all[:, 512 * i:512 * (i + 1)] for i in range(8)]
        bank_free = [None] * 8

        s_const = P.dsem("s_const")
        s_w = [P.dsem(f"s_w{i}") for i in range(NWBUF)]
        s_pw = P.dsem("s_pw")
        s_x = P.dsem("s_x")
        s_xc = [P.dsem(f"s_xc{i}") for i in range(2)]
        s_xs = [P.dsem(f"s_xs{i}") for i in range(2)]
        s_send = [P.dsem(f"s_send{i}") for i in range(3)]
        s_cck = [P.dsem(f"s_cck{i}") for i in range(2)]
        s_ccv = [P.dsem(f"s_ccv{i}") for i in range(2)]
        s_par = [P.dsem(f"s_par{i}") for i in range(4)]
        s_gst = [P.dsem(f"s_gst{i}") for i in range(2)]
        s_gld = [P.dsem(f"s_gld{i}") for i in range(4)]
        s_hal = P.dsem("s_hal")
        s_out = P.dsem("s_out")
        s_cc = [P.dsem(f"s_cc{i}") for i in range(3)]

        for dst, srcd in ((gains, gains_d), (pscale, pscale_d), (flags, flags_d), (ones_f, ones_d)):
            P.dma("pool", dst[:, :], srcd[:, :], s_const)
        P.dma("pool", invcnt[:, :, :], invcnt_d.ap().rearrange("p (a b) -> p a b", a=4), s_const)
        for dst, srcd in ((negtri, negtri_d), (negones, negones_d), (diagm, diag_d)):
            P.dma("pool", dst[:, :], srcd[:, :], s_const)
        const_ev = (s_const.h, s_const.cnt)
        eps_col = flags[:, 7:8]

        wlist = []
        for l in range(depth):
            def wi(g, l=l):
                return (("in", l, g), w_in_l[l][0 if DEBUG_TINY else g], 16 * GW)
            seq = []
            seq += [wi(16 + g) for g in range(4)]
            seq += [wi(4 + g) for g in range(4)]
            for g in range(4):
                seq += [wi(20 + g), wi(8 + g)]
            seq += [wi(0), wi(12)]
            for hp in range(4):
                if hp < 3:
                    seq += [wi(hp + 1), wi(12 + hp + 1)]
                for mg in (2 * hp, 2 * hp + 1):
                    seq += [wi(24 + mg), wi(32 + mg)]
            for mg in range(8):
                seq += [(("au", l, mg), w_au_r[0 if DEBUG_TINY else l * 8 + mg], 8 * GW),
                        (("pu", l, mg), w_pu_r[0 if DEBUG_TINY else l * 8 + mg], 8 * GW)]
            for og in range(8):
                seq += [(("out", l, og), w_out_r[0 if DEBUG_TINY else l * 8 + og], 16 * GW)]
            wlist += seq
        wstate = {"issued": 0, "next": 0, "free": {}, "ev": {}}

        def w_try_issue():
            while wstate["issued"] < len(wlist):
                i = wstate["issued"]
                if i >= NWBUF and (i - NWBUF) not in wstate["free"]:
                    break
                b = i % NWBUF
                _, srcap, n = wlist[i]
                fr = wstate["free"].get(i - NWBUF)
                wstate["ev"][i] = P.dma("pool", wbufs[b][:, 0:n], srcap, s_w[b], waits=[fr],
                                         max_dma_last_dim=8192)
                wstate["issued"] += 1

        def w_next(kcn, key=None):
            i = wstate["next"]
            wstate["next"] += 1
            if key is not None:
                assert wlist[i][0] == key, (wlist[i][0], key)
            w_try_issue()
            assert i in wstate["ev"], "weight load not issued (too many live groups)"
            b = i % NWBUF
            W = wbufs[b][:, 0:kcn * GW].rearrange("p (k j) -> p k j", k=kcn)
            return W, wstate["ev"][i], i

        def w_release(i, ev):
            wstate["free"][i] = ev
            w_try_issue()

        pbank_rr = {"i": 0}

        def next_bank(lo, hi):
            b = lo + pbank_rr["i"] % (hi - lo)
            pbank_rr["i"] += 1
            return b

        def proj_fm(W, wev, blk, kcn, rhs_fn, evac_fn, extra_waits=(), chunk_waits=None):
            pe_ev = None
            for ci, (t0, t1) in enumerate(PCH):
                n = t1 - t0
                b = next_bank(0, 6)
                for kc in range(kcn):
                    waits = []
                    if kc == 0:
                        waits = [wev, bank_free[b]] + list(extra_waits)
                        if chunk_waits:
                            waits += list(chunk_waits[ci])
                    pe_ev = I("pe", "matmul", waits, banks[b][:, 0:n], lhsT=W[:, kc, blk * 128:(blk + 1) * 128],
                              rhs=rhs_fn(kc, t0, t1), start=(kc == 0), stop=(kc == kcn - 1), sig=(kc == kcn - 1))
                bank_free[b] = evac_fn(banks[b][:, 0:n], ci, t0, t1, pe_ev)
            return pe_ev

        def hT_rhs(kc, t0, t1):
            return hT[:, kc, t0:t1]

        def evac_copy(dst_fn, scale=None, extra=(), eng=None):
            def f(ps, ci, t0, t1, pe_ev):
                w = [pe_ev] + list(extra)
                if (ci % 2 == 0 and eng is None) or eng == "dve":
                    if scale is None:
                        return I("dve", "tensor_copy", w, out=dst_fn(t0, t1), in_=ps)
                    return I("dve", "tensor_scalar", w, out=dst_fn(t0, t1), in0=ps, scalar1=scale,
                             scalar2=None, op0=ALU.mult)
                return I("act", "activation", w, out=dst_fn(t0, t1), in_=ps, func=AF.Copy,
                         scale=(1.0 if scale is None else scale))
            return f

        def evac_act(dst_fn, func, extra_fn=None):
            def f(ps, ci, t0, t1, pe_ev):
                w = [pe_ev] + (list(extra_fn()) if extra_fn else [])
                return I("act", "activation", w, out=dst_fn(t0, t1), in_=ps, func=func)
            return f

        def rmsnorm_stats(stg, n, par, ld):
            b = 6 + par
            sq_free = [None, None]
            pe_ev = None
            for c in range(KC):
                a_ev = I("act", "activation", [ld, sq_free[c % 2], const_ev], out=sqb[:, c % 2, 0:n],
                         in_=stg[:, c, 0:n], func=AF.Square)
                pe_ev = I("pe", "matmul", [a_ev, bank_free[b] if c == 0 else None], banks[b][:, 0:n],
                          lhsT=negones[:, :], rhs=sqb[:, c % 2, 0:n],
                          start=(c == 0), stop=(c == KC - 1))
                sq_free[c % 2] = pe_ev
            r1 = I("act", "activation", [pe_ev], out=rs_tmp[par][:, 0:n], in_=banks[b][:, 0:n], func=AF.Sqrt,
                   bias=eps_col, scale=-1.0 / D)
            r2 = I("dve", "reciprocal", [r1], out=rstd[par][:, 0:n], in_=rs_tmp[par][:, 0:n])
            bank_free[b] = r1
            return r2

        def emit_layer(l):
            src = xT0 if l == 0 else xres
            P.barrier()
            pw_ev = P.dma("pool", pool_w_sb[:, :], pool_w_r[l], s_pw, waits=[P.last("pe")],
                          max_dma_last_dim=8192)

            norm_done = {}
            for ci, (t0, t1) in enumerate(NCH):
                n = t1 - t0
                par = ci % 2
                stg = xst_pre if ci == 0 else xst[par]
                ld = P.dma("sp", stg[:, :, 0:n], src[:, :, t0:t1].rearrange("c p t -> p c t"), s_x,
                           waits=[norm_done.get(ci - 2)])
                r2 = rmsnorm_stats(stg, n, par, ld)
                for c in range(KC):
                    norm_done[ci] = I("dve", "scalar_tensor_tensor", [r2, ld], out=hT[:, c, t0:t1],
                                      in0=stg[:, c, 0:n], scalar=gains[:, l * 16 + c:l * 16 + c + 1],
                                      in1=rstd[par][:, 0:n], op0=ALU.mult, op1=ALU.mult)
            def _cov(t0, t1):
                return [norm_done[ci] for ci, (a, b) in enumerate(NCH) if a < t1 and b > t0]
            hT_chunk_ready = [_cov(t0, t1) for (t0, t1) in PCH]
            normA_done = norm_done[len(NCH) - 1]
            if DEBUG_STOP == "A":
                P.barrier()
                return

            z_ev = I("dve", "memset", [], UBUF[:, :, 0:16], 0.0)
            for g in range(4):
                W, wev, wi_ = w_next(16)
                for blk in range(2):
                    ub = 2 * g + blk
                    pe_ev = proj_fm(W, wev, blk, 16, hT_rhs,
                                    evac_copy(lambda t0, t1, ub=ub: UBUF[:, ub, 16 + t0:16 + t1]),
                                    chunk_waits=(hT_chunk_ready if (g == 0 and blk == 0) else None))
                w_release(wi_, pe_ev)
            u_done = [P.last("dve"), P.last("act")]
            if DEBUG_STOP == "Ba":
                P.barrier()
                return
            su = P.dma("sp", sendU[l].ap().rearrange("p (a b) -> p a b", a=8), UBUF[:, :, T:T + 16],
                       s_send[0], waits=u_done)
            cc_u_ev = P.cc(s_cc[0], [su], kind="AllGather", op=ALU.bypass, replica_groups=GROUPS,
                           ins=[sendU[l].ap().opt()], outs=[recvU[l].ap().opt()])
            if DEBUG_STOP == "Bb":
                P.barrier(extra=[cc_u_ev])
                return

            hl = P.dma("sp", uhalo[:, :, :], recvU[l][0:128, :].rearrange("p (a b) -> p a b", a=8), s_hal,
                       waits=[cc_u_ev])
            h1 = I("dve", "tensor_scalar", [hl, const_ev], out=uhalo[:, :, :], in0=uhalo[:, :, :],
                   scalar1=flags[:, 5:6], scalar2=None, op0=ALU.mult)
            h2 = I("dve", "scalar_tensor_tensor", [h1, z_ev] + u_done, out=UBUF[:, :, 16:32], in0=UBUF[:, :, 16:32],
                   scalar=flags[:, 4:5], in1=uhalo[:, :, :], op0=ALU.mult, op1=ALU.add)

            for g in range(4):
                W, wev, wi_ = w_next(16, ("in", l, 4 + g))
                for blk in range(2):
                    h = 2 * g + blk
                    pe_ev = proj_fm(W, wev, blk, 16, hT_rhs,
                                    evac_copy(lambda t0, t1, h=h: KT_own[:, h, t0:t1], eng="act",
                                              extra=[normA_done]))
                w_release(wi_, pe_ev)
            k_done = [P.last("act")]
            cc_k_ev = []
            for a in range(2):
                sk = P.dma("sp", sendK[l][a].ap().rearrange("(h p) t -> p h t", p=128),
                           KT_own[:, 4 * a:4 * a + 4, :], s_send[1], waits=k_done)
                cc_k_ev.append(P.cc(s_cck[a], [sk], kind="AllGather", op=ALU.bypass, replica_groups=GROUPS,
                                    ins=[sendK[l][a].ap().opt()], outs=[recvK[l][a].ap().opt()]))

            tblocks = [(0, 16)] + [(16 + 128 * i, 16 + 128 * (i + 1)) for i in range(NB)]
            opool_done = {}
            cc_v_ev = []
            for pg in range(4):
                w = POOL_W[pg]
                sgp = sgp2[pg % 2]
                for j in range(2):
                    ub = 2 * pg + j
                    u = UBUF[:, ub, :]
                    cur = u
                    d = 1
                    tgl = 0
                    ev = h2
                    while d < w:
                        dst = PA if tgl == 0 else PBb
                        ev = I("dve", "tensor_tensor", [ev], out=dst[:, d:16 + T], in0=cur[:, d:16 + T],
                               in1=cur[:, 0:16 + T - d], op=ALU.add)
                        cur = dst
                        d *= 2
                        tgl ^= 1
                    I("dve", "scalar_tensor_tensor", [ev], out=pooledbf[:, j, 16:T], in0=cur[:, 32:16 + T],
                      scalar=1.0 / w, in1=u[:, 32:16 + T], op0=ALU.mult, op1=ALU.subtract)
                    t16 = I("dve", "tensor_tensor", [ev, const_ev], out=tmp16[:, :], in0=cur[:, 16:32],
                            in1=invcnt[:, pg, :], op=ALU.mult)
                    I("dve", "tensor_tensor", [t16], out=pooledbf[:, j, 0:16], in0=tmp16[:, :], in1=u[:, 16:32],
                      op=ALU.subtract)
                pl_ev = P.last("dve")
                W, wev, wi_ = w_next(16, ("in", l, 20 + pg))
                for blk in range(2):
                    pe_ev = proj_fm(W, wev, blk, 16, hT_rhs,
                                    evac_act(lambda t0, t1, blk=blk, sgp=sgp: sgp[:, blk, t0:t1], AF.Silu,
                                             extra_fn=lambda pg=pg: [opool_done.get(pg - 2)]))
                w_release(wi_, pe_ev)
                sgp_ev = P.last("act")
                W, wev, wi_ = w_next(16, ("in", l, 8 + pg))
                pe_ev = None
                for tb, (t0, t1) in enumerate(tblocks):
                    m = t1 - t0
                    b = next_bank(0, 6)
                    for kc in range(KC):
                        pe_ev = I("pe", "matmul", ([wev, bank_free[b]] if kc == 0 else []), banks[b][0:m, 0:GW],
                                  lhsT=hT[:, kc, t0:t1], rhs=W[:, kc, :], start=(kc == 0), stop=(kc == KC - 1),
                                  sig=(kc == KC - 1))
                    bank_free[b] = I("act", "activation", [pe_ev], out=V_own[0:m, tb, pg * GW:(pg + 1) * GW],
                                     in_=banks[b][0:m, 0:GW], func=AF.Copy)
                w_release(wi_, pe_ev)
                if pg % 2 == 1:
                    a = pg // 2
                    sv = P.dma("sp", sendV[l][a].ap().rearrange("(b p) c -> p b c", p=128),
                               V_own[:, :, 512 * a:512 * (a + 1)], s_send[2], waits=[P.last("act")])
                    cc_v_ev.append(P.cc(s_ccv[a], [sv], kind="AllGather", op=ALU.bypass, replica_groups=GROUPS,
                                        ins=[sendV[l][a].ap().opt()], outs=[recvV[l][a].ap().opt()]))
                for ob in range(2):
                    for ci, (t0, t1) in enumerate(PCH):
                        n = t1 - t0
                        b = next_bank(0, 6)
                        for kc in range(2):
                            o0 = (pg * 2 + kc) * 256 + ob * 128
                            pe_ev = I("pe", "matmul", ([pl_ev, pw_ev, bank_free[b]] if kc == 0 else []),
                                      banks[b][:, 0:n], lhsT=pool_w_sb[:, o0:o0 + 128], rhs=pooledbf[:, kc, t0:t1],
                                      start=(kc == 0), stop=(kc == 1), sig=(kc == 1))
                        cidx = l * 8 + 2 * pg + ob
                        bank_free[b] = I("dve", "scalar_tensor_tensor", [pe_ev, sgp_ev, const_ev],
                                         out=o_pool[:, 2 * pg + ob, t0:t1], in0=banks[b][:, 0:n],
                                         scalar=pscale[:, cidx:cidx + 1], in1=sgp[:, ob, t0:t1],
                                         op0=ALU.mult, op1=ALU.mult)
                opool_done[pg] = P.last("dve")
            P.barrier()
            if DEBUG_STOP == "B":
                return

            def qga_tasks(hp, bank_rng):
                tasks = []
                stt = {}

                def mk(kind, blk, ci):
                    def run():
                        if (kind, "W") not in stt:
                            stt[(kind, "W")] = w_next(16, ("in", l, (hp if kind == "q" else 12 + hp)))
                        W, wev, wi_ = stt[(kind, "W")]
                        t0, t1 = PCH[ci]
                        n = t1 - t0
                        b = next_bank(*bank_rng)
                        pe_ev = None
                        for kc in range(KC):
                            pe_ev = I("pe", "matmul", ([wev, bank_free[b]] if kc == 0 else []), banks[b][:, 0:n],
                                      lhsT=W[:, kc, blk * 128:(blk + 1) * 128], rhs=hT[:, kc, t0:t1],
                                      start=(kc == 0), stop=(kc == KC - 1), sig=(kc == KC - 1))
                        if kind == "q":
                            bank_free[b] = I("dve", "tensor_scalar", [pe_ev], out=QT2[hp % 2][:, blk, t0:t1],
                                             in0=banks[b][:, 0:n], scalar1=float(DH) ** -0.5, scalar2=None,
                                             op0=ALU.mult)
                        else:
                            bank_free[b] = I("dve", "tensor_copy", [pe_ev], out=SGA2[hp % 2][:, blk, t0:t1],
                                             in_=banks[b][:, 0:n])
                        if blk == 1 and ci == len(PCH) - 1:
                            w_release(wi_, pe_ev)
                    return run
                for kind in ("q", "g"):
                    for blk in range(2):
                        for ci in range(len(PCH)):
                            tasks.append(mk(kind, blk, ci))
                return tasks

            gst_state = {"free": [None, None], "n": 0, "store": {}}

            def gate_tasks(hp):
                tasks = []
                stt = {}

                def mk(kind, mg, blk, ci):
                    def run():
                        wk = (kind, mg)
                        if wk not in stt:
                            stt[wk] = w_next(16, ("in", l, (24 if kind == 0 else 32) + mg))
                        W, wev, wi_ = stt[wk]
                        t0, t1 = PCH[ci]
                        n = t1 - t0
                        if ci == 0:
                            stt["sidx"] = gst_state["n"] % 2
                            gst_state["n"] += 1
                            stt["evs"] = []
                        sidx = stt["sidx"]
                        b = 7
                        pe_ev = None
                        for kc in range(KC):
                            pe_ev = I("pe", "matmul", ([wev, bank_free[b]] if kc == 0 else []), banks[b][:, 0:n],
                                      lhsT=W[:, kc, blk * 128:(blk + 1) * 128], rhs=hT[:, kc, t0:t1],
                                      start=(kc == 0), stop=(kc == KC - 1), sig=(kc == KC - 1))
                        bank_free[b] = I("dve", "tensor_copy", [pe_ev, gst_state["free"][sidx] if ci == 0 else None],
                                         out=GST[sidx][:, t0:t1], in_=banks[b][:, 0:n])
                        stt["evs"].append(bank_free[b])
                        if ci == len(PCH) - 1:
                            c = 2 * mg + blk
                            ev = P.dma("sp", gate_d[l][kind][c], GST[sidx][:, :], s_gst[sidx], waits=stt["evs"])
                            gst_state["free"][sidx] = ev
                            gst_state["store"][(kind, c)] = ev
                            if blk == 1:
                                w_release(wi_, pe_ev)
                    return run
                for mg in (2 * hp, 2 * hp + 1):
                    for kind in (0, 1):
                        for blk in range(2):
                            for ci in range(len(PCH)):
                                tasks.append(mk(kind, mg, blk, ci))
                return tasks

            for t_ in qga_tasks(0, (0, 6)):
                t_()

            def par_loads(hp, free_ev):
                evs = []
                for hh in range(2):
                    h = 2 * hp + hh
                    slot = h % 4
                    ha, hl_ = h // 4, h % 4
                    P.dma("sp", KT_par[:, slot, :], recvK[l][ha][hl_ * 128:(hl_ + 1) * 128, :], s_par[slot],
                          waits=[cc_k_ev[ha], free_ev])
                    e2 = P.dma("sp", V_par[:, slot, :].rearrange("p (b c) -> p b c", b=9),
                               recvV[l][ha][0:9 * 128, hl_ * 128:(hl_ + 1) * 128].rearrange("(b p) c -> p b c", p=128),
                               s_par[slot], waits=[cc_v_ev[ha], free_ev])
                    evs.append(e2)
                return evs

            pair_done = {}
            par_evs = {0: par_loads(0, None)}
            for hp in range(4):
                if hp < 3:
                    par_evs[hp + 1] = par_loads(hp + 1, pair_done.get(hp - 1))
                P.barrier(engs=("pe", "act", "dve"))
                att_ctx["par"] = par_evs[hp]
                attention_pair(hp, filler=((qga_tasks(hp + 1, (7, 8)) if hp < 3 else []) + gate_tasks(hp)))
                pair_done[hp] = P.last("pe")
            P.barrier()
            if DEBUG_STOP == "C":
                return
            def load_gate(kind, mg, blk, free_ev):
                c = 2 * mg + blk
                buf = (SGA if kind == 0 else SGP)[blk]
                ld = P.dma("sp", buf[:, :], gate_d[l][kind][c], s_gld[kind * 2 + blk],
                           waits=[gst_state["store"][(kind, c)], free_ev])
                return I("act", "activation", [ld], out=buf[:, :], in_=buf[:, :], func=AF.Sigmoid)

            ga_ev = {}
            gp_ev = {}
            for blk in range(2):
                ga_ev[(0, blk)] = load_gate(0, 0, blk, gst_state["free"][blk])
                gp_ev[(0, blk)] = load_gate(1, 0, blk, None)
            for mg in range(8):
                W, wev, wi_ = w_next(8, ("au", l, mg))
                for blk in range(2):
                    def ev_ya(ps, ci, t0, t1, pe_ev, blk=blk):
                        return I("dve", "tensor_tensor", [pe_ev, ga_ev[(mg, blk)]], out=T1[blk][:, t0:t1],
                                 in0=ps, in1=SGA[blk][:, t0:t1], op=ALU.mult)
                    pe_ev = proj_fm(W, wev, blk, 8, lambda kc, t0, t1: o_attn[:, kc, t0:t1], ev_ya)
                    if mg < 7:
                        ga_ev[(mg + 1, blk)] = load_gate(0, mg + 1, blk, P.last("dve"))
                w_release(wi_, pe_ev)
                W, wev, wi_ = w_next(8, ("pu", l, mg))
                for blk in range(2):
                    c = 2 * mg + blk

                    def ev_yp(ps, ci, t0, t1, pe_ev, blk=blk, c=c):
                        e1 = I("dve", "tensor_tensor", [pe_ev, gp_ev[(mg, blk)]], out=SGP[blk][:, t0:t1], in0=ps,
                               in1=SGP[blk][:, t0:t1], op=ALU.mult)
                        return I("dve", "tensor_tensor", [e1], out=merged[:, c, t0:t1], in0=T1[blk][:, t0:t1],
                                 in1=SGP[blk][:, t0:t1], op=ALU.add)
                    pe_ev = proj_fm(W, wev, blk, 8, lambda kc, t0, t1: o_pool[:, kc, t0:t1], ev_yp)
                    if mg < 7:
                        gp_ev[(mg + 1, blk)] = load_gate(1, mg + 1, blk, P.last("dve"))
                w_release(wi_, pe_ev)
            P.barrier()
            if DEBUG_STOP == "D":
                return

            xs_ev = [None, None]
            for og in range(8):
                W, wev, wi_ = w_next(16)
                for blk in range(2):
                    c = 2 * og + blk
                    xb = c % 2
                    ld = P.dma("sp", xc[:, xb, :], src[c], s_xc[xb], waits=[xs_ev[xb]])

                    def ev_res(ps, ci, t0, t1, pe_ev, xb=xb, ld=ld):
                        return I("dve", "tensor_tensor", [pe_ev, ld], out=xc[:, xb, t0:t1], in0=ps,
                                 in1=xc[:, xb, t0:t1], op=ALU.add)
                    pe_ev = proj_fm(W, wev, blk, 16, lambda kc, t0, t1: merged[:, kc, t0:t1], ev_res)
                    xs_ev[xb] = P.dma("sp", xres[c], xc[:, xb, :], s_xs[xb], waits=[P.last("dve")])
                w_release(wi_, pe_ev)
            P.barrier(extra=[xs_ev[0], xs_ev[1]])

        att_ctx = {}

        def attention_pair(hp, filler=None):
            QTc = QT2[hp % 2]
            SGc = SGA2[hp % 2]
            sg_ev = I("act", "activation", [], out=SGc[:, :, :], in_=SGc[:, :, :], func=AF.Silu)
            units = []
            chunk_id = 0
            for hh in range(2):
                h = 2 * hp + hh

                def own_blk(kb):
                    return (KT_own[:, h, 16 + 128 * kb:16 + 128 * (kb + 1)],
                            V_own[:, 1 + kb, h * 128:(h + 1) * 128], 128)

                def own_pre():
                    return (KT_own[:, h, 0:16], V_own[0:16, 0, h * 128:(h + 1) * 128], 16)

                slot = h % 4

                def par_blk(kb):
                    return (KT_par[:, slot, 16 + 128 * kb:16 + 128 * (kb + 1)],
                            V_par[:, slot, (1 + kb) * 128:(2 + kb) * 128], 128)

                def par_pre():
                    return (KT_par[:, slot, 0:16], V_par[0:16, slot, 0:128], 16)

                for c in range(2):
                    q0 = 16 + 512 * c
                    ul = []
                    for i in (3, 2, 1, 0):
                        kt, v, nk = own_blk(4 * c + i)
                        ul.append(dict(kt=kt, v=v, nk=nk, c0=128 * i, N=512 - 128 * i, vis=None, diag=True))
                    for kb in range(4 * c - 1, -1, -1):
                        kt, v, nk = own_blk(kb)
                        ul.append(dict(kt=kt, v=v, nk=nk, c0=0, N=512, vis=None, diag=False))
                    kt, v, nk = own_pre()
                    ul.append(dict(kt=kt, v=v, nk=nk, c0=0, N=512, vis=0, diag=False))
                    for kb in range(NB - 1, -1, -1):
                        kt, v, nk = par_blk(kb)
                        ul.append(dict(kt=kt, v=v, nk=nk, c0=0, N=512, vis=1, diag=False, pw=att_ctx["par"][hh]))
                    kt, v, nk = par_pre()
                    ul.append(dict(kt=kt, v=v, nk=nk, c0=0, N=512, vis=1, diag=False, pw=att_ctx["par"][hh]))
                    for k, u_ in enumerate(ul):
                        u_.update(q0=q0, hh=hh, h=h, first=(k == 0), last=(k == len(ul) - 1), chunk=chunk_id, qn=512)
                    units += ul
                    chunk_id += 1
                kt, v, nk = own_pre()
                units.append(dict(kt=kt, v=v, nk=nk, c0=0, N=16, vis=None, diag=True, q0=0, hh=hh, h=h,
                                  first=True, last=True, chunk=chunk_id, qn=16))
                chunk_id += 1

            nU = len(units)
            evE = [None] * nU
            evSP = [None] * nU
            evY = [None] * nU
            evARG = [None] * nU
            evR = [None] * nU
            evA = [None] * nU
            evAV = [None] * nU
            o_free = {0: None, 1: None}
            rz_ev = {}
            SB_ = (0, 1, 2)
            YB_ = (3, 4)
            OB_ = (5, 6)

            def stage0(i):
                u_ = units[i]
                nk, c0, N = u_["nk"], u_["c0"], u_["N"]
                sbk = SB_[i % 3]
                qap = QTc[:, u_["hh"], u_["q0"] + c0:u_["q0"] + c0 + N]
                ev_s = I("pe", "matmul", [evARG[i - 3] if i >= 3 else None, u_.get("pw")], banks[sbk][0:nk, c0:c0 + N],
                         lhsT=u_["kt"], rhs=qap, start=True, stop=False, skip_group_check=True)
                evE[i] = I("act", "activation", [ev_s], out=Ebuf[0:nk, 0:N], in_=banks[sbk][0:nk, c0:c0 + N],
                           func=AF.Exp)

            def stage1(i):
                u_ = units[i]
                nk, c0, N = u_["nk"], u_["c0"], u_["N"]
                sc = 1.0 if u_["vis"] is None else flags[0:nk, u_["vis"]:u_["vis"] + 1]
                ev_sp = I("act", "activation", [evE[i], evY[i - 3] if i >= 3 else None, const_ev],
                          out=SPb[i % 3][0:nk, 0:N], in_=Ebuf[0:nk, 0:N], func=AF.Ln,
                          bias=1.0, scale=sc)
                if u_["diag"]:
                    dn = min(128, N)
                    ev_sp = I("dve", "tensor_tensor", [ev_sp, const_ev], out=SPb[i % 3][0:nk, 0:dn],
                              in0=SPb[i % 3][0:nk, 0:dn], in1=diagm[0:nk, 0:dn], op=ALU.mult)
                evSP[i] = ev_sp

            def stage2(i):
                u_ = units[i]
                nk, c0, N = u_["nk"], u_["c0"], u_["N"]
                sbk = SB_[i % 3]
                ybk = YB_[i % 2]
                rb = Rb[u_["chunk"] % 2]
                I("pe", "matmul", [evSP[i], const_ev], banks[sbk][0:nk, c0:c0 + N], lhsT=negtri[0:nk, 0:nk],
                  rhs=SPb[i % 3][0:nk, 0:N], start=False, stop=True, skip_group_check=True, sig=False)
                evY[i] = I("pe", "matmul", [evR[i - 2] if i >= 2 else None], banks[ybk][:, c0:c0 + N],
                           lhsT=negones[0:nk, :], rhs=SPb[i % 3][0:nk, 0:N], start=True, stop=True)
                if u_["first"]:
                    rz_ev[u_["chunk"]] = I("dve", "memset", [], rb[:, :], 0.0)
                evARG[i] = I("dve", "tensor_tensor",
                             [evY[i], evA[i - 3] if i >= 3 else None, rz_ev[u_["chunk"]],
                              evR[i - 1] if i >= 1 else None],
                             out=ARGb[i % 3][0:nk, 0:N], in0=banks[sbk][0:nk, c0:c0 + N], in1=rb[0:nk, c0:c0 + N],
                             op=ALU.add)
                evR[i] = I("dve", "tensor_tensor", [evY[i]], out=rb[:, c0:c0 + N], in0=banks[ybk][:, c0:c0 + N],
                           in1=rb[:, c0:c0 + N], op=ALU.add)

            def stage2b(i):
                u_ = units[i]
                nk, c0, N = u_["nk"], u_["c0"], u_["N"]
                wts = [evARG[i], evAV[i - 3] if i >= 3 else None]
                if u_["vis"] is None:
                    ev_a = I("act", "activation", wts, out=Ab[i % 3][0:nk, 0:N], in_=ARGb[i % 3][0:nk, 0:N],
                             func=AF.Exp)
                else:
                    ev_a = I("act", "activation", wts, out=Ab[i % 3][0:nk, 0:N], in_=ARGb[i % 3][0:nk, 0:N],
                             func=AF.Exp, bias=flags[0:nk, 2 + u_["vis"]:3 + u_["vis"]])
                if u_["diag"]:
                    dn = min(128, N)
                    ev_a = I("dve", "tensor_tensor", [ev_a], out=Ab[i % 3][0:nk, 0:dn], in0=Ab[i % 3][0:nk, 0:dn],
                             in1=diagm[0:nk, 0:dn], op=ALU.mult)
                evA[i] = ev_a

            def stage3(i):
                u_ = units[i]
                nk, c0, N = u_["nk"], u_["c0"], u_["N"]
                ob = OB_[u_["chunk"] % 2]
                evAV[i] = I("pe", "matmul", [evA[i], o_free[u_["chunk"] % 2] if u_["first"] else None],
                            banks[ob][:, c0:c0 + N], lhsT=u_["v"], rhs=Ab[i % 3][0:nk, 0:N], start=u_["first"],
                            stop=u_["last"], skip_group_check=True)
                if u_["last"]:
                    qn, q0, hh, h = u_["qn"], u_["q0"], u_["hh"], u_["h"]
                    o_free[u_["chunk"] % 2] = I("dve", "tensor_tensor", [evAV[i], sg_ev], out=o_attn[:, h, q0:q0 + qn],
                                                in0=banks[ob][:, 0:qn], in1=SGc[:, hh, q0:q0 + qn], op=ALU.mult)

            nF = len(filler) if filler else 0
            fpos = [int((k + 0.5) * nU / nF) for k in range(nF)]
            for it in range(nU + 4):
                if 0 <= it - 4 < nU:
                    stage3(it - 4)
                if 0 <= it - 3 < nU:
                    stage2b(it - 3)
                if 0 <= it - 2 < nU:
                    stage2(it - 2)
                if 0 <= it - 1 < nU:
                    stage1(it - 1)
                if it < nU:
                    stage0(it)
                while filler and fpos and fpos[0] <= it:
                    fpos.pop(0)
                    filler.pop(0)()
            while filler:
                filler.pop(0)()

        for l in range(depth):
            emit_layer(l)

        yfree = [None, None]
        norm_done = {}
        fsrc = xT0 if DEBUG_STOP else xres
        for ci, (t0, t1) in enumerate(NCH[1:]):
            n = t1 - t0
            par = ci % 2
            ld = P.dma("sp", xst[par][:, :, 0:n], fsrc[:, :, t0:t1].rearrange("c p t -> p c t"), s_x,
                       waits=[norm_done.get(ci - 2)])
            r2 = rmsnorm_stats(xst[par], n, par, ld)
            for c in range(KC):
                yb = c % 2
                y_ev = I("dve", "scalar_tensor_tensor", [r2, ld, yfree[yb]], out=yst[:, yb, 0:n],
                         in0=xst[par][:, c, 0:n], scalar=gains[:, NL * 16 + c:NL * 16 + c + 1],
                         in1=rstd[par][:, 0:n], op0=ALU.mult, op1=ALU.mult)
                norm_done[ci] = y_ev
                yfree[yb] = P.dma("sp", outT[c, :, t0 - 16:t1 - 16], yst[:, yb, 0:n], s_out, waits=[y_ev])
        fin = (s_out.h, s_out.cnt)

        with nc.Block() as block:
            @block.tensor
            def _(e):
                P.replay("pe", e)

            @block.scalar
            def _(e):
                P.replay("act", e)

            @block.vector
            def _(e):
                P.replay("dve", e)

            @block.gpsimd
            def _(e):
                P.replay("pool", e)

            @block.sync
            def _(e):
                P.replay("sp", e)
                e.wait_ge(fin[0], fin[1])
    return nc


def _prep_inputs(x, meta_tokens, norm_gain, w_in, pool_w, pool_scale, w_attn_up, w_pool_up, w_out, final_gain, depth=NL):
    f32 = np.float32
    x = np.asarray(x, f32)
    B = x.shape[0]

    def regroup(w, kcn, ngrp):
        w = np.asarray(w, f32).reshape(NL, kcn, 128, ngrp, GW)
        return np.ascontiguousarray(w.transpose(0, 3, 2, 1, 4)).reshape(NL * ngrp, 128, kcn * GW)

    wir = regroup(w_in, 16, 40).reshape(NL, 40, 128, 16 * GW)
    shared = {f"w_in_r{l}": wir[l] for l in range(depth)}
    shared.update({
        "w_au_r": regroup(w_attn_up, 8, 8)[:depth * 8],
        "w_pu_r": regroup(w_pool_up, 8, 8)[:depth * 8],
        "w_out_r": regroup(w_out, 16, 8)[:depth * 8],
    })
    pw = np.asarray(pool_w, f32).reshape(NL, 4, 2, 128, 256)
    shared["pool_w_r"] = np.ascontiguousarray(pw.transpose(0, 3, 1, 2, 4)).reshape(NL, 128, 4 * 2 * 256)[:depth]
    g = np.concatenate([np.asarray(norm_gain, f32).reshape(NL, 16, 128), np.asarray(final_gain, f32).reshape(1, 16, 128)], 0)
    shared["gains"] = np.ascontiguousarray(g.transpose(2, 0, 1)).reshape(128, NL * 16 + 16)
    ps = np.asarray(pool_scale, f32).reshape(NL, 8, 128)
    shared["pscale"] = np.ascontiguousarray(ps.transpose(2, 0, 1)).reshape(128, NL * 8)
    shared["ones_f32"] = np.ones((128, 128), f32)
    j = np.arange(128)
    shared["negtri"] = -(j[:, None] >= j[None, :]).astype(f32)
    shared["negones"] = -np.ones((128, 128), f32)
    shared["diagmask"] = (j[:, None] < j[None, :]).astype(f32)

    if DEBUG_TINY:
        for k in list(shared):
            if k.startswith("w_"):
                shared[k] = np.ascontiguousarray(shared[k][:1])
    in_maps = []
    for r in range(N_CORES):
        b, half = r // 2, r % 2
        tok = np.zeros((T, D), f32)
        if half == 0:
            tok[0:PRE] = np.asarray(meta_tokens, f32)
        tok[PRE:] = x[b, half * RT:(half + 1) * RT]
        xT0 = np.ascontiguousarray(tok.T).reshape(16, 128, T)
        fl = np.zeros((128, 8), f32)
        if half == 0:
            fl[:, 0] = 1.0; fl[:, 1] = 0.0; fl[:, 2] = 0.0; fl[:, 3] = NEG; fl[:, 4] = 1.0; fl[:, 5] = 0.0
        else:
            fl[:, 0] = 0.0; fl[:, 1] = 1.0; fl[:, 2] = NEG; fl[:, 3] = 0.0; fl[:, 4] = 0.0; fl[:, 5] = 1.0
        fl[:, 6] = 1.0
        fl[:, 7] = EPS
        ic = np.zeros((128, 4, 16), f32)
        for gi, w in enumerate(POOL_W):
            if half == 0:
                ic[:, gi, :] = 1.0 / np.minimum(np.arange(16) + 1, w)
            else:
                ic[:, gi, :] = 1.0 / w
        m = dict(shared)
        m["xT0"] = xT0
        m["flags"] = fl
        m["invcnt"] = ic.reshape(128, 64)
        in_maps.append(m)
    return in_maps


_NC_CACHE = {}


def run(inputs, depth=NL):
    if depth not in _NC_CACHE:
        _NC_CACHE[depth] = build(depth)
    nc = _NC_CACHE[depth]
    in_maps = _prep_inputs(depth=depth, **inputs)
    res = run_bass_kernel_spmd(nc, in_maps, core_ids=list(range(N_CORES)))
    B = inputs["x"].shape[0]
    out = np.zeros((B, 2 * RT, D), np.float32)
    for r in range(N_CORES):
        b, half = r // 2, r % 2
        o = np.asarray(res.results[r]["outT"]).reshape(D, RT)
        out[b, half * RT:(half + 1) * RT, :] = o.T
    return out


def kernel(x, meta_tokens, norm_gain, w_in, pool_w, pool_scale, w_attn_up, w_pool_up, w_out, final_gain):
    return run(dict(x=x, meta_tokens=meta_tokens, norm_gain=norm_gain, w_in=w_in, pool_w=pool_w,
                    pool_scale=pool_scale, w_attn_up=w_attn_up, w_pool_up=w_pool_up, w_out=w_out,
                    final_gain=final_gain))
```

```python
import numpy as np
from contextlib import ExitStack
import concourse.bass as bass
import concourse.mybir as mybir
from concourse.bass_utils import run_bass_kernel_spmd

F32 = mybir.dt.float32
BF16 = mybir.dt.bfloat16
F32R = mybir.dt.float32r
AF = mybir.ActivationFunctionType
ALU = mybir.AluOpType

D = 2048
KC = 16
NL = 4
H = 8
DH = 128
PRE = 16
RT = 1024
T = PRE + RT
NB = RT // 128
GW = 256
PCH = [(0, 347), (347, 694), (694, 1040)]
NCH = [(0, 16)] + [(16 + 256 * i, 16 + 256 * (i + 1)) for i in range(4)]
POOL_W = (2, 4, 8, 16)
EPS = 1e-6
NEG = -30000.0
N_CORES = 8
NWBUF = 3
DEBUG_STOP = None
DEBUG_TINY = False


class DSem:
    def __init__(self, h):
        self.h = h
        self.cnt = 0


class Prog:
    ENG = ("pe", "act", "dve", "pool", "sp")

    def __init__(self, nc, st):
        self.nc = nc
        self.st = st
        self.q = {k: [] for k in self.ENG}
        self.sem = {k: st.enter_context(nc.semaphore("sem_" + k)) for k in self.ENG}
        self.cnt = {k: 0 for k in self.ENG}

    def dsem(self, name):
        return DSem(self.st.enter_context(self.nc.semaphore(name)))

    def I(self, eng, name, waits, *args, sig=True, **kw):
        ev = None
        if sig:
            self.cnt[eng] += 1
            ev = (self.sem[eng], self.cnt[eng])
        self.q[eng].append((name, args, kw, [w for w in waits if w is not None], ev, 1))
        return ev

    def dma(self, eng, out, in_, sem, waits=(), **kw):
        sem.cnt += 16
        ev = (sem.h, sem.cnt)
        kw = dict(kw)
        kw.update(out=out, in_=in_)
        self.q[eng].append(("dma_start", (), kw, [w for w in waits if w is not None], ev, 16))
        return ev

    def cc(self, sem, waits, **kw):
        sem.cnt += 1
        ev = (sem.h, sem.cnt)
        self.q["pool"].append(("collective_compute", (), kw, [w for w in waits if w is not None], ev, 1))
        return ev

    def last(self, eng):
        return (self.sem[eng], self.cnt[eng]) if self.cnt[eng] > 0 else None

    def barrier(self, engs=("pe", "act", "dve", "sp"), extra=()):
        evs = [self.last(o) for o in engs] + list(extra)
        for e in engs:
            self.q[e].append((None, (), {}, [w for w in evs if w is not None and w[0] is not self.sem[e]], None, 0))

    def replay(self, eng, e):
        waited = {}
        for name, args, kw, waits, ev, inc in self.q[eng]:
            for (h, v) in waits:
                k = h.num
                if waited.get(k, 0) >= v:
                    continue
                e.wait_ge(h, v)
                waited[k] = v
            if name is None:
                continue
            ins = getattr(e, name)(*args, **kw)
            if ev is not None:
                ins.then_inc(ev[0], inc)


def build(depth=NL):
    nc = bass.Bass("TRN2", target_bir_lowering=False)

    def din(name, shape, dt=F32):
        return nc.dram_tensor(name, shape, dt, kind="ExternalInput")

    xT0 = din("xT0", [16, 128, T])
    NG = 1 if DEBUG_TINY else 40
    w_in_l = [din(f"w_in_r{l}", [NG, 128, 16 * GW]) for l in range(depth)]
    NG8 = 1 if DEBUG_TINY else depth * 8
    w_au_r = din("w_au_r", [NG8, 128, 8 * GW])
    w_pu_r = din("w_pu_r", [NG8, 128, 8 * GW])
    w_out_r = din("w_out_r", [NG8, 128, 16 * GW])
    pool_w_r = din("pool_w_r", [depth, 128, 4 * 2 * 256])
    gains_d = din("gains", [128, NL * 16 + 16])
    pscale_d = din("pscale", [128, NL * 8])
    flags_d = din("flags", [128, 8])
    invcnt_d = din("invcnt", [128, 4 * 16])
    ones_d = din("ones_f32", [128, 128])
    negtri_d = din("negtri", [128, 128])
    negones_d = din("negones", [128, 128])
    diag_d = din("diagmask", [128, 128])
    outT = nc.dram_tensor("outT", [16, 128, RT], F32, kind="ExternalOutput")

    xres = nc.dram_tensor("xres", [16, 128, T], F32)
    sendK = [[nc.dram_tensor(f"sendK{l}_{a}", [4 * 128, T], BF16) for a in range(2)] for l in range(depth)]
    recvK = [[nc.dram_tensor(f"recvK{l}_{a}", [2 * 4 * 128, T], BF16) for a in range(2)] for l in range(depth)]
    sendV = [[nc.dram_tensor(f"sendV{l}_{a}", [9 * 128, 512], BF16) for a in range(2)] for l in range(depth)]
    recvV = [[nc.dram_tensor(f"recvV{l}_{a}", [2 * 9 * 128, 512], BF16) for a in range(2)] for l in range(depth)]
    sendU = [nc.dram_tensor(f"sendU{l}", [128, 128], F32) for l in range(depth)]
    recvU = [nc.dram_tensor(f"recvU{l}", [256, 128], F32) for l in range(depth)]
    gate_d = [[nc.dram_tensor(f"gate{l}_{k}", [16, 128, T], F32) for k in range(2)] for l in range(depth)]
    GROUPS = [[0, 1], [2, 3], [4, 5], [6, 7]]

    with ExitStack() as st:
        def sb(name, shape, dt):
            return st.enter_context(nc.sbuf_tensor(name, shape, dt))

        P = Prog(nc, st)
        I = P.I

        hT = sb("hT", [128, KC, T], BF16)
        RA = sb("RA", [128, 8768], F32)
        xst_pre = sb("xst_pre", [128, KC, 16], F32)
        RB = sb("RB", [128, 12928], F32)
        o_pool = sb("o_pool", [128, 8, T], BF16)
        wbufs = [sb(f"wbuf{i}", [128, 16 * GW], BF16) for i in range(NWBUF)]
        pool_w_sb = sb("pool_w_sb", [128, 4 * 2 * 256], BF16)
        RC = sb("RC", [128, 6240], F32)
        RD = sb("RD", [128, 4384], F32)
        gains = sb("gains_sb", [128, NL * 16 + 16], F32)
        pscale = sb("pscale_sb", [128, NL * 8], F32)
        flags = sb("flags_sb", [128, 8], F32)
        invcnt = sb("invcnt_sb", [128, 4, 16], F32)
        ones_f = sb("ones_sb", [128, 128], F32)
        negtri = sb("negtri_sb", [128, 128], BF16)
        negones = sb("negones_sb", [128, 128], BF16)
        diagm = sb("diag_sb", [128, 128], BF16)

        def view(region, boff, dt, shape):
            esz = 4 if dt == F32 else 2
            n = int(np.prod(shape[1:])) * esz
            ap = region[:, boff // 4:(boff + n) // 4]
            if dt != F32:
                ap = ap.bitcast(dt)
            if len(shape) == 3:
                ap = ap.rearrange("p (a b) -> p a b", a=shape[1])
            return ap

        xst = [view(RA, i * 16384, F32, [128, KC, 256]) for i in range(2)]
        KT_own = view(RA, 0, BF16, [128, H, T])
        V_own = view(RA, 16640, BF16, [128, 9, 1024])
        merged = view(RA, 0, BF16, [128, KC, T])
        UBUF = view(RB, 0, F32, [128, 8, 16 + T])
        o_attn = view(RB, 0, BF16, [128, H, T])
        ASC = 16640
        Ebuf = view(RB, ASC, F32, [128, 512])
        ARGb = [view(RB, ASC + 2048 + i * 2048, F32, [128, 512]) for i in range(3)]
        Rb = [view(RB, ASC + 8192 + i * 2048, F32, [128, 512]) for i in range(2)]
        SPb = [view(RB, ASC + 12288 + i * 1024, BF16, [128, 512]) for i in range(3)]
        Ab = [view(RB, ASC + 15360 + i * 1024, BF16, [128, 512]) for i in range(3)]
        QT2 = [view(RB, ASC + 18432 + i * 8320, BF16, [128, 2, T]) for i in range(2)]
        SGA2 = [view(RB, ASC + 18432 + 4160 + i * 8320, BF16, [128, 2, T]) for i in range(2)]
        PA = view(RC, 0, F32, [128, 16 + T])
        PBb = view(RC, 4224, F32, [128, 16 + T])
        pooledbf = view(RC, 8448, BF16, [128, 2, T])
        sgp2 = [view(RC, 12608, BF16, [128, 2, T]), view(RC, 17344, BF16, [128, 2, T])]
        uhalo = view(RC, 16768, F32, [128, 8, 16])
        tmp16 = view(RC, 17280, F32, [128, 16])
        SGA = [view(RC, i * 4160, F32, [128, T]) for i in range(2)]
        SGP = [view(RC, 8320 + i * 4160, F32, [128, T]) for i in range(2)]
        T1 = [view(RC, 16640 + i * 4160, F32, [128, T]) for i in range(2)]
        GST = [view(RC, i * 4160, F32, [128, T]) for i in range(2)]
        yst = view(RC, 0, F32, [128, 2, 512])
        KT_par = view(RD, 0, BF16, [128, 4, T])
        V_par = view(RD, 8320, BF16, [128, 4, 9 * 128])
        xc = view(RD, 0, F32, [128, 2, T])
        sq = view(RD, 0, F32, [128, 2, 512])
        sqb = view(RD, 0, BF16, [128, 2, 512])
        rs_tmp = [view(RD, 4096 + i * 1024, F32, [128, 256]) for i in range(2)]
        rstd = [view(RD, 6144 + i * 1024, F32, [128, 256]) for i in range(2)]

        ps_all = st.enter_context(nc.psum_tensor("ps_all", [128, 4096], F32))
        banks = [ps_all[:, 512 * i:512 * (i + 1)] for i in range(8)]
        bank_free = [None] * 8

        s_const = P.dsem("s_const")
        s_w = [P.dsem(f"s_w{i}") for i in range(NWBUF)]
        s_pw = P.dsem("s_pw")
        s_x = [P.dsem(f"s_x{i}") for i in range(3)]
        s_xc = [P.dsem(f"s_xc{i}") for i in range(2)]
        s_xs = [P.dsem(f"s_xs{i}") for i in range(2)]
        s_send = [P.dsem(f"s_send{i}") for i in range(5)]
        s_cck = [P.dsem(f"s_cck{i}") for i in range(2)]
        s_ccv = [P.dsem(f"s_ccv{i}") for i in range(2)]
        s_par = [P.dsem(f"s_par{i}") for i in range(4)]
        s_gst = [P.dsem(f"s_gst{i}") for i in range(2)]
        s_gld = [P.dsem(f"s_gld{i}") for i in range(4)]
        s_hal = P.dsem("s_hal")
        s_out = [P.dsem(f"s_out{i}") for i in range(2)]
        s_cc = [P.dsem(f"s_cc{i}") for i in range(3)]

        for dst, srcd in ((gains, gains_d), (pscale, pscale_d), (flags, flags_d), (ones_f, ones_d)):
            P.dma("pool", dst[:, :], srcd[:, :], s_const)
        P.dma("pool", invcnt[:, :, :], invcnt_d.ap().rearrange("p (a b) -> p a b", a=4), s_const)
        for dst, srcd in ((negtri, negtri_d), (negones, negones_d), (diagm, diag_d)):
            P.dma("pool", dst[:, :], srcd[:, :], s_const)
        const_ev = (s_const.h, s_const.cnt)
        eps_col = flags[:, 7:8]

        wlist = []
        for l in range(depth):
            def wi(g, l=l):
                return (("in", l, g), w_in_l[l][0 if DEBUG_TINY else g], 16 * GW)
            seq = []
            seq += [wi(16 + g) for g in range(4)]
            seq += [wi(4 + g) for g in range(4)]
            for g in range(4):
                seq += [wi(20 + g), wi(8 + g)]
            seq += [wi(0), wi(12)]
            for hp in range(4):
                if hp < 3:
                    seq += [wi(hp + 1), wi(12 + hp + 1)]
                for mg in (2 * hp, 2 * hp + 1):
                    seq += [wi(24 + mg), wi(32 + mg)]
            for mg in range(8):
                seq += [(("au", l, mg), w_au_r[0 if DEBUG_TINY else l * 8 + mg], 8 * GW),
                        (("pu", l, mg), w_pu_r[0 if DEBUG_TINY else l * 8 + mg], 8 * GW)]
            for og in range(8):
                seq += [(("out", l, og), w_out_r[0 if DEBUG_TINY else l * 8 + og], 16 * GW)]
            wlist += seq
        wstate = {"issued": 0, "next": 0, "free": {}, "ev": {}}

        def w_try_issue():
            while wstate["issued"] < len(wlist):
                i = wstate["issued"]
                if i >= NWBUF and (i - NWBUF) not in wstate["free"]:
                    break
                b = i % NWBUF
                _, srcap, n = wlist[i]
                fr = wstate["free"].get(i - NWBUF)
                wstate["ev"][i] = P.dma("pool", wbufs[b][:, 0:n], srcap, s_w[b], waits=[fr],
                                         max_dma_last_dim=8192)
                wstate["issued"] += 1

        def w_next(kcn, key=None):
            i = wstate["next"]
            wstate["next"] += 1
            if key is not None:
                assert wlist[i][0] == key, (wlist[i][0], key)
            w_try_issue()
            assert i in wstate["ev"], "weight load not issued (too many live groups)"
            b = i % NWBUF
            W = wbufs[b][:, 0:kcn * GW].rearrange("p (k j) -> p k j", k=kcn)
            return W, wstate["ev"][i], i

        def w_release(i, ev):
            wstate["free"][i] = ev
            w_try_issue()

        pbank_rr = {"i": 0}

        def next_bank(lo, hi):
            b = lo + pbank_rr["i"] % (hi - lo)
            pbank_rr["i"] += 1
            return b

        def proj_fm(W, wev, blk, kcn, rhs_fn, evac_fn, extra_waits=(), chunk_waits=None):
            pe_ev = None
            for ci, (t0, t1) in enumerate(PCH):
                n = t1 - t0
                b = next_bank(0, 6)
                for kc in range(kcn):
                    waits = []
                    if kc == 0:
                        waits = [wev, bank_free[b]] + list(extra_waits)
                        if chunk_waits:
                            waits += list(chunk_waits[ci])
                    pe_ev = I("pe", "matmul", waits, banks[b][:, 0:n], lhsT=W[:, kc, blk * 128:(blk + 1) * 128],
                              rhs=rhs_fn(kc, t0, t1), start=(kc == 0), stop=(kc == kcn - 1), sig=(kc == kcn - 1))
                bank_free[b] = evac_fn(banks[b][:, 0:n], ci, t0, t1, pe_ev)
            return pe_ev

        def hT_rhs(kc, t0, t1):
            return hT[:, kc, t0:t1]

        def evac_copy(dst_fn, scale=None, extra=(), eng=None):
            def f(ps, ci, t0, t1, pe_ev):
                w = [pe_ev] + list(extra)
                if (ci % 2 == 0 and eng is None) or eng == "dve":
                    if scale is None:
                        return I("dve", "tensor_copy", w, out=dst_fn(t0, t1), in_=ps)
                    return I("dve", "tensor_scalar", w, out=dst_fn(t0, t1), in0=ps, scalar1=scale,
                             scalar2=None, op0=ALU.mult)
                return I("act", "activation", w, out=dst_fn(t0, t1), in_=ps, func=AF.Copy,
                         scale=(1.0 if scale is None else scale))
            return f

        def evac_act(dst_fn, func, extra_fn=None):
            def f(ps, ci, t0, t1, pe_ev):
                w = [pe_ev] + (list(extra_fn()) if extra_fn else [])
                return I("act", "activation", w, out=dst_fn(t0, t1), in_=ps, func=func)
            return f

        def rmsnorm_stats(stg, n, par, ld):
            b = 6 + par
            sq_free = [None, None]
            pe_ev = None
            for c in range(KC):
                a_ev = I("act", "activation", [ld, sq_free[c % 2], const_ev], out=sqb[:, c % 2, 0:n],
                         in_=stg[:, c, 0:n], func=AF.Square)
                pe_ev = I("pe", "matmul", [a_ev, bank_free[b] if c == 0 else None], banks[b][:, 0:n],
                          lhsT=negones[:, :], rhs=sqb[:, c % 2, 0:n],
                          start=(c == 0), stop=(c == KC - 1))
                sq_free[c % 2] = pe_ev
            r1 = I("act", "activation", [pe_ev], out=rs_tmp[par][:, 0:n], in_=banks[b][:, 0:n], func=AF.Sqrt,
                   bias=eps_col, scale=-1.0 / D)
            r2 = I("dve", "reciprocal", [r1], out=rstd[par][:, 0:n], in_=rs_tmp[par][:, 0:n])
            bank_free[b] = r1
            return r2

        def emit_layer(l):
            src = xT0 if l == 0 else xres
            P.barrier()
            pw_ev = P.dma("pool", pool_w_sb[:, :], pool_w_r[l], s_pw, waits=[P.last("pe")],
                          max_dma_last_dim=8192)

            norm_done = {}
            for ci, (t0, t1) in enumerate(NCH):
                n = t1 - t0
                par = ci % 2
                stg = xst_pre if ci == 0 else xst[par]
                ld = P.dma("sp", stg[:, :, 0:n], src[:, :, t0:t1].rearrange("c p t -> p c t"),
                           s_x[2 if ci == 0 else par], waits=[norm_done.get(ci - 2)])
                r2 = rmsnorm_stats(stg, n, par, ld)
                for c in range(KC):
                    norm_done[ci] = I("dve", "scalar_tensor_tensor", [r2, ld], out=hT[:, c, t0:t1],
                                      in0=stg[:, c, 0:n], scalar=gains[:, l * 16 + c:l * 16 + c + 1],
                                      in1=rstd[par][:, 0:n], op0=ALU.mult, op1=ALU.mult)
            def _cov(t0, t1):
                return [norm_done[ci] for ci, (a, b) in enumerate(NCH) if a < t1 and b > t0]
            hT_chunk_ready = [_cov(t0, t1) for (t0, t1) in PCH]
            normA_done = norm_done[len(NCH) - 1]
            if DEBUG_STOP == "A":
                P.barrier()
                return

            z_ev = I("dve", "memset", [], UBUF[:, :, 0:16], 0.0)
            for g in range(4):
                W, wev, wi_ = w_next(16)
                for blk in range(2):
                    ub = 2 * g + blk
                    pe_ev = proj_fm(W, wev, blk, 16, hT_rhs,
                                    evac_copy(lambda t0, t1, ub=ub: UBUF[:, ub, 16 + t0:16 + t1]),
                                    chunk_waits=(hT_chunk_ready if (g == 0 and blk == 0) else None))
                w_release(wi_, pe_ev)
            u_done = [P.last("dve"), P.last("act")]
            if DEBUG_STOP == "Ba":
                P.barrier()
                return
            su = P.dma("sp", sendU[l].ap().rearrange("p (a b) -> p a b", a=8), UBUF[:, :, T:T + 16],
                       s_send[0], waits=u_done)
            cc_u_ev = P.cc(s_cc[0], [su], kind="AllGather", op=ALU.bypass, replica_groups=GROUPS,
                           ins=[sendU[l].ap().opt()], outs=[recvU[l].ap().opt()])
            if DEBUG_STOP == "Bb":
                P.barrier(extra=[cc_u_ev])
                return

            hl = P.dma("sp", uhalo[:, :, :], recvU[l][0:128, :].rearrange("p (a b) -> p a b", a=8), s_hal,
                       waits=[cc_u_ev])
            h1 = I("dve", "tensor_scalar", [hl, const_ev], out=uhalo[:, :, :], in0=uhalo[:, :, :],
                   scalar1=flags[:, 5:6], scalar2=None, op0=ALU.mult)
            h2 = I("dve", "scalar_tensor_tensor", [h1, z_ev] + u_done, out=UBUF[:, :, 16:32], in0=UBUF[:, :, 16:32],
                   scalar=flags[:, 4:5], in1=uhalo[:, :, :], op0=ALU.mult, op1=ALU.add)

            for g in range(4):
                W, wev, wi_ = w_next(16, ("in", l, 4 + g))
                for blk in range(2):
                    h = 2 * g + blk
                    pe_ev = proj_fm(W, wev, blk, 16, hT_rhs,
                                    evac_copy(lambda t0, t1, h=h: KT_own[:, h, t0:t1], eng="act",
                                              extra=[normA_done]))
                w_release(wi_, pe_ev)
            k_done = [P.last("act")]
            cc_k_ev = []
            for a in range(2):
                sk = P.dma("sp", sendK[l][a].ap().rearrange("(h p) t -> p h t", p=128),
                           KT_own[:, 4 * a:4 * a + 4, :], s_send[1 + a], waits=k_done)
                cc_k_ev.append(P.cc(s_cck[a], [sk], kind="AllGather", op=ALU.bypass, replica_groups=GROUPS,
                                    ins=[sendK[l][a].ap().opt()], outs=[recvK[l][a].ap().opt()]))

            tblocks = [(0, 16)] + [(16 + 128 * i, 16 + 128 * (i + 1)) for i in range(NB)]
            opool_done = {}
            cc_v_ev = []
            for pg in range(4):
                w = POOL_W[pg]
                sgp = sgp2[pg % 2]
                for j in range(2):
                    ub = 2 * pg + j
                    u = UBUF[:, ub, :]
                    cur = u
                    d = 1
                    tgl = 0
                    ev = h2
                    while d < w:
                        dst = PA if tgl == 0 else PBb
                        ev = I("dve", "tensor_tensor", [ev], out=dst[:, d:16 + T], in0=cur[:, d:16 + T],
                               in1=cur[:, 0:16 + T - d], op=ALU.add)
                        cur = dst
                        d *= 2
                        tgl ^= 1
                    I("dve", "scalar_tensor_tensor", [ev], out=pooledbf[:, j, 16:T], in0=cur[:, 32:16 + T],
                      scalar=1.0 / w, in1=u[:, 32:16 + T], op0=ALU.mult, op1=ALU.subtract)
                    t16 = I("dve", "tensor_tensor", [ev, const_ev], out=tmp16[:, :], in0=cur[:, 16:32],
                            in1=invcnt[:, pg, :], op=ALU.mult)
                    I("dve", "tensor_tensor", [t16], out=pooledbf[:, j, 0:16], in0=tmp16[:, :], in1=u[:, 16:32],
                      op=ALU.subtract)
                pl_ev = P.last("dve")
                W, wev, wi_ = w_next(16, ("in", l, 20 + pg))
                for blk in range(2):
                    pe_ev = proj_fm(W, wev, blk, 16, hT_rhs,
                                    evac_act(lambda t0, t1, blk=blk, sgp=sgp: sgp[:, blk, t0:t1], AF.Silu,
                                             extra_fn=lambda pg=pg: [opool_done.get(pg - 2)]))
                w_release(wi_, pe_ev)
                sgp_ev = P.last("act")
                W, wev, wi_ = w_next(16, ("in", l, 8 + pg))
                pe_ev = None
                for tb, (t0, t1) in enumerate(tblocks):
                    m = t1 - t0
                    b = next_bank(0, 6)
                    for kc in range(KC):
                        pe_ev = I("pe", "matmul", ([wev, bank_free[b]] if kc == 0 else []), banks[b][0:m, 0:GW],
                                  lhsT=hT[:, kc, t0:t1], rhs=W[:, kc, :], start=(kc == 0), stop=(kc == KC - 1),
                                  sig=(kc == KC - 1))
                    bank_free[b] = I("act", "activation", [pe_ev], out=V_own[0:m, tb, pg * GW:(pg + 1) * GW],
                                     in_=banks[b][0:m, 0:GW], func=AF.Copy)
                w_release(wi_, pe_ev)
                if pg % 2 == 1:
                    a = pg // 2
                    sv = P.dma("sp", sendV[l][a].ap().rearrange("(b p) c -> p b c", p=128),
                               V_own[:, :, 512 * a:512 * (a + 1)], s_send[3 + a], waits=[P.last("act")])
                    cc_v_ev.append(P.cc(s_ccv[a], [sv], kind="AllGather", op=ALU.bypass, replica_groups=GROUPS,
                                        ins=[sendV[l][a].ap().opt()], outs=[recvV[l][a].ap().opt()]))
                for ob in range(2):
                    for ci, (t0, t1) in enumerate(PCH):
                        n = t1 - t0
                        b = next_bank(0, 6)
                        for kc in range(2):
                            o0 = (pg * 2 + kc) * 256 + ob * 128
                            pe_ev = I("pe", "matmul", ([pl_ev, pw_ev, bank_free[b]] if kc == 0 else []),
                                      banks[b][:, 0:n], lhsT=pool_w_sb[:, o0:o0 + 128], rhs=pooledbf[:, kc, t0:t1],
                                      start=(kc == 0), stop=(kc == 1), sig=(kc == 1))
                        cidx = l * 8 + 2 * pg + ob
                        bank_free[b] = I("dve", "scalar_tensor_tensor", [pe_ev, sgp_ev, const_ev],
                                         out=o_pool[:, 2 * pg + ob, t0:t1], in0=banks[b][:, 0:n],
                                         scalar=pscale[:, cidx:cidx + 1], in1=sgp[:, ob, t0:t1],
                                         op0=ALU.mult, op1=ALU.mult)
                opool_done[pg] = P.last("dve")
            P.barrier()
            if DEBUG_STOP == "B":
                return

            def qga_tasks(hp, bank_rng):
                tasks = []
                stt = {}

                def mk(kind, blk, ci):
                    def run():
                        if (kind, "W") not in stt:
                            stt[(kind, "W")] = w_next(16, ("in", l, (hp if kind == "q" else 12 + hp)))
                        W, wev, wi_ = stt[(kind, "W")]
                        t0, t1 = PCH[ci]
                        n = t1 - t0
                        b = next_bank(*bank_rng)
                        pe_ev = None
                        for kc in range(KC):
                            pe_ev = I("pe", "matmul", ([wev, bank_free[b]] if kc == 0 else []), banks[b][:, 0:n],
                                      lhsT=W[:, kc, blk * 128:(blk + 1) * 128], rhs=hT[:, kc, t0:t1],
                                      start=(kc == 0), stop=(kc == KC - 1), sig=(kc == KC - 1))
                        if kind == "q":
                            bank_free[b] = I("dve", "tensor_scalar", [pe_ev], out=QT2[hp % 2][:, blk, t0:t1],
                                             in0=banks[b][:, 0:n], scalar1=float(DH) ** -0.5, scalar2=None,
                                             op0=ALU.mult)
                        else:
                            bank_free[b] = I("dve", "tensor_copy", [pe_ev], out=SGA2[hp % 2][:, blk, t0:t1],
                                             in_=banks[b][:, 0:n])
                        if blk == 1 and ci == len(PCH) - 1:
                            w_release(wi_, pe_ev)
                    return run
                for kind in ("q", "g"):
                    for blk in range(2):
                        for ci in range(len(PCH)):
                            tasks.append(mk(kind, blk, ci))
                return tasks

            gst_state = {"free": [None, None], "n": 0, "store": {}}

            def gate_tasks(hp):
                tasks = []
                stt = {}

                def mk(kind, mg, blk, ci):
                    def run():
                        wk = (kind, mg)
                        if wk not in stt:
                            stt[wk] = w_next(16, ("in", l, (24 if kind == 0 else 32) + mg))
                        W, wev, wi_ = stt[wk]
                        t0, t1 = PCH[ci]
                        n = t1 - t0
                        if ci == 0:
                            stt["sidx"] = gst_state["n"] % 2
                            gst_state["n"] += 1
                            stt["evs"] = []
                        sidx = stt["sidx"]
                        b = 7
                        pe_ev = None
                        for kc in range(KC):
                            pe_ev = I("pe", "matmul", ([wev, bank_free[b]] if kc == 0 else []), banks[b][:, 0:n],
                                      lhsT=W[:, kc, blk * 128:(blk + 1) * 128], rhs=hT[:, kc, t0:t1],
                                      start=(kc == 0), stop=(kc == KC - 1), sig=(kc == KC - 1))
                        bank_free[b] = I("dve", "tensor_copy", [pe_ev, gst_state["free"][sidx] if ci == 0 else None],
                                         out=GST[sidx][:, t0:t1], in_=banks[b][:, 0:n])
                        stt["evs"].append(bank_free[b])
                        if ci == len(PCH) - 1:
                            c = 2 * mg + blk
                            ev = P.dma("sp", gate_d[l][kind][c], GST[sidx][:, :], s_gst[sidx], waits=stt["evs"])
                            gst_state["free"][sidx] = ev
                            gst_state["store"][(kind, c)] = ev
                            if blk == 1:
                                w_release(wi_, pe_ev)
                    return run
                for mg in (2 * hp, 2 * hp + 1):
                    for kind in (0, 1):
                        for blk in range(2):
                            for ci in range(len(PCH)):
                                tasks.append(mk(kind, mg, blk, ci))
                return tasks

            for t_ in qga_tasks(0, (0, 6)):
                t_()

            def par_loads(hp, free_ev):
                evs = []
                for hh in range(2):
                    h = 2 * hp + hh
                    slot = h % 4
                    ha, hl_ = h // 4, h % 4
                    P.dma("sp", KT_par[:, slot, :], recvK[l][ha][hl_ * 128:(hl_ + 1) * 128, :], s_par[slot],
                          waits=[cc_k_ev[ha], free_ev])
                    e2 = P.dma("sp", V_par[:, slot, :].rearrange("p (b c) -> p b c", b=9),
                               recvV[l][ha][0:9 * 128, hl_ * 128:(hl_ + 1) * 128].rearrange("(b p) c -> p b c", p=128),
                               s_par[slot], waits=[cc_v_ev[ha], free_ev])
                    evs.append(e2)
                return evs

            pair_done = {}
            par_evs = {0: par_loads(0, None)}
            for hp in range(4):
                if hp < 3:
                    par_evs[hp + 1] = par_loads(hp + 1, pair_done.get(hp - 1))
                P.barrier(engs=("pe", "act", "dve"))
                att_ctx["par"] = par_evs[hp]
                attention_pair(hp, filler=((qga_tasks(hp + 1, (7, 8)) if hp < 3 else []) + gate_tasks(hp)))
                pair_done[hp] = P.last("pe")
            P.barrier()
            if DEBUG_STOP == "C":
                return
            def load_gate(kind, mg, blk, free_ev):
                c = 2 * mg + blk
                buf = (SGA if kind == 0 else SGP)[blk]
                ld = P.dma("sp", buf[:, :], gate_d[l][kind][c], s_gld[kind * 2 + blk],
                           waits=[gst_state["store"][(kind, c)], free_ev])
                return I("act", "activation", [ld], out=buf[:, :], in_=buf[:, :], func=AF.Sigmoid)

            ga_ev = {}
            gp_ev = {}
            for blk in range(2):
                ga_ev[(0, blk)] = load_gate(0, 0, blk, gst_state["free"][blk])
                gp_ev[(0, blk)] = load_gate(1, 0, blk, None)
            for mg in range(8):
                W, wev, wi_ = w_next(8, ("au", l, mg))
                for blk in range(2):
                    def ev_ya(ps, ci, t0, t1, pe_ev, blk=blk):
                        return I("dve", "tensor_tensor", [pe_ev, ga_ev[(mg, blk)]], out=T1[blk][:, t0:t1],
                                 in0=ps, in1=SGA[blk][:, t0:t1], op=ALU.mult)
                    pe_ev = proj_fm(W, wev, blk, 8, lambda kc, t0, t1: o_attn[:, kc, t0:t1], ev_ya)
                    if mg < 7:
                        ga_ev[(mg + 1, blk)] = load_gate(0, mg + 1, blk, P.last("dve"))
                w_release(wi_, pe_ev)
                W, wev, wi_ = w_next(8, ("pu", l, mg))
                for blk in range(2):
                    c = 2 * mg + blk

                    def ev_yp(ps, ci, t0, t1, pe_ev, blk=blk, c=c):
                        e1 = I("dve", "tensor_tensor", [pe_ev, gp_ev[(mg, blk)]], out=SGP[blk][:, t0:t1], in0=ps,
                               in1=SGP[blk][:, t0:t1], op=ALU.mult)
                        return I("dve", "tensor_tensor", [e1], out=merged[:, c, t0:t1], in0=T1[blk][:, t0:t1],
                                 in1=SGP[blk][:, t0:t1], op=ALU.add)
                    pe_ev = proj_fm(W, wev, blk, 8, lambda kc, t0, t1: o_pool[:, kc, t0:t1], ev_yp)
                    if mg < 7:
                        gp_ev[(mg + 1, blk)] = load_gate(1, mg + 1, blk, P.last("dve"))
                w_release(wi_, pe_ev)
            P.barrier()
            if DEBUG_STOP == "D":
                return

            xs_ev = [None, None]
            for og in range(8):
                W, wev, wi_ = w_next(16)
                for blk in range(2):
                    c = 2 * og + blk
                    xb = c % 2
                    ld = P.dma("sp", xc[:, xb, :], src[c], s_xc[xb], waits=[xs_ev[xb]])

                    def ev_res(ps, ci, t0, t1, pe_ev, xb=xb, ld=ld):
                        return I("dve", "tensor_tensor", [pe_ev, ld], out=xc[:, xb, t0:t1], in0=ps,
                                 in1=xc[:, xb, t0:t1], op=ALU.add)
                    pe_ev = proj_fm(W, wev, blk, 16, lambda kc, t0, t1: merged[:, kc, t0:t1], ev_res)
                    xs_ev[xb] = P.dma("sp", xres[c], xc[:, xb, :], s_xs[xb], waits=[P.last("dve")])
                w_release(wi_, pe_ev)
            P.barrier(extra=[xs_ev[0], xs_ev[1]])

        att_ctx = {}

        def attention_pair(hp, filler=None):
            QTc = QT2[hp % 2]
            SGc = SGA2[hp % 2]
            sg_ev = I("act", "activation", [], out=SGc[:, :, :], in_=SGc[:, :, :], func=AF.Silu)
            units = []
            chunk_id = 0
            for hh in range(2):
                h = 2 * hp + hh

                def own_blk(kb):
                    return (KT_own[:, h, 16 + 128 * kb:16 + 128 * (kb + 1)],
                            V_own[:, 1 + kb, h * 128:(h + 1) * 128], 128)

                def own_pre():
                    return (KT_own[:, h, 0:16], V_own[0:16, 0, h * 128:(h + 1) * 128], 16)

                slot = h % 4

                def par_blk(kb):
                    return (KT_par[:, slot, 16 + 128 * kb:16 + 128 * (kb + 1)],
                            V_par[:, slot, (1 + kb) * 128:(2 + kb) * 128], 128)

                def par_pre():
                    return (KT_par[:, slot, 0:16], V_par[0:16, slot, 0:128], 16)

                for c in range(2):
                    q0 = 16 + 512 * c
                    ul = []
                    for i in (3, 2, 1, 0):
                        kt, v, nk = own_blk(4 * c + i)
                        ul.append(dict(kt=kt, v=v, nk=nk, c0=128 * i, N=512 - 128 * i, vis=None, diag=True))
                    for kb in range(4 * c - 1, -1, -1):
                        kt, v, nk = own_blk(kb)
                        ul.append(dict(kt=kt, v=v, nk=nk, c0=0, N=512, vis=None, diag=False))
                    kt, v, nk = own_pre()
                    ul.append(dict(kt=kt, v=v, nk=nk, c0=0, N=512, vis=0, diag=False))
                    for kb in range(NB - 1, -1, -1):
                        kt, v, nk = par_blk(kb)
                        ul.append(dict(kt=kt, v=v, nk=nk, c0=0, N=512, vis=1, diag=False, pw=att_ctx["par"][hh]))
                    kt, v, nk = par_pre()
                    ul.append(dict(kt=kt, v=v, nk=nk, c0=0, N=512, vis=1, diag=False, pw=att_ctx["par"][hh]))
                    for k, u_ in enumerate(ul):
                        u_.update(q0=q0, hh=hh, h=h, first=(k == 0), last=(k == len(ul) - 1), chunk=chunk_id, qn=512)
                    units += ul
                    chunk_id += 1
                kt, v, nk = own_pre()
                units.append(dict(kt=kt, v=v, nk=nk, c0=0, N=16, vis=None, diag=True, q0=0, hh=hh, h=h,
                                  first=True, last=True, chunk=chunk_id, qn=16))
                chunk_id += 1

            nU = len(units)
            evE = [None] * nU
            evSP = [None] * nU
            evY = [None] * nU
            evARG = [None] * nU
            evR = [None] * nU
            evA = [None] * nU
            evAV = [None] * nU
            o_free = {0: None, 1: None}
            rz_ev = {}
            SB_ = (0, 1, 2)
            YB_ = (3, 4)
            OB_ = (5, 6)

            def stage0(i):
                u_ = units[i]
                nk, c0, N = u_["nk"], u_["c0"], u_["N"]
                sbk = SB_[i % 3]
                qap = QTc[:, u_["hh"], u_["q0"] + c0:u_["q0"] + c0 + N]
                ev_s = I("pe", "matmul", [evARG[i - 3] if i >= 3 else None, u_.get("pw")], banks[sbk][0:nk, c0:c0 + N],
                         lhsT=u_["kt"], rhs=qap, start=True, stop=False, skip_group_check=True)
                evE[i] = I("act", "activation", [ev_s], out=Ebuf[0:nk, 0:N], in_=banks[sbk][0:nk, c0:c0 + N],
                           func=AF.Exp)

            def stage1(i):
                u_ = units[i]
                nk, c0, N = u_["nk"], u_["c0"], u_["N"]
                sc = 1.0 if u_["vis"] is None else flags[0:nk, u_["vis"]:u_["vis"] + 1]
                ev_sp = I("act", "activation", [evE[i], evY[i - 3] if i >= 3 else None, const_ev],
                          out=SPb[i % 3][0:nk, 0:N], in_=Ebuf[0:nk, 0:N], func=AF.Ln,
                          bias=1.0, scale=sc)
                if u_["diag"]:
                    dn = min(128, N)
                    ev_sp = I("dve", "tensor_tensor", [ev_sp, const_ev], out=SPb[i % 3][0:nk, 0:dn],
                              in0=SPb[i % 3][0:nk, 0:dn], in1=diagm[0:nk, 0:dn], op=ALU.mult)
                evSP[i] = ev_sp

            def stage2(i):
                u_ = units[i]
                nk, c0, N = u_["nk"], u_["c0"], u_["N"]
                sbk = SB_[i % 3]
                ybk = YB_[i % 2]
                rb = Rb[u_["chunk"] % 2]
                I("pe", "matmul", [evSP[i], const_ev], banks[sbk][0:nk, c0:c0 + N], lhsT=negtri[0:nk, 0:nk],
                  rhs=SPb[i % 3][0:nk, 0:N], start=False, stop=True, skip_group_check=True, sig=False)
                evY[i] = I("pe", "matmul", [evR[i - 2] if i >= 2 else None], banks[ybk][:, c0:c0 + N],
                           lhsT=negones[0:nk, :], rhs=SPb[i % 3][0:nk, 0:N], start=True, stop=True)
                if u_["first"]:
                    rz_ev[u_["chunk"]] = I("dve", "memset", [], rb[:, :], 0.0)
                evARG[i] = I("dve", "tensor_tensor",
                             [evY[i], evA[i - 3] if i >= 3 else None, rz_ev[u_["chunk"]],
                              evR[i - 1] if i >= 1 else None],
                             out=ARGb[i % 3][0:nk, 0:N], in0=banks[sbk][0:nk, c0:c0 + N], in1=rb[0:nk, c0:c0 + N],
                             op=ALU.add)
                evR[i] = I("dve", "tensor_tensor", [evY[i]], out=rb[:, c0:c0 + N], in0=banks[ybk][:, c0:c0 + N],
                           in1=rb[:, c0:c0 + N], op=ALU.add)

            def stage2b(i):
                u_ = units[i]
                nk, c0, N = u_["nk"], u_["c0"], u_["N"]
                wts = [evARG[i], evAV[i - 3] if i >= 3 else None]
                if u_["vis"] is None:
                    ev_a = I("act", "activation", wts, out=Ab[i % 3][0:nk, 0:N], in_=ARGb[i % 3][0:nk, 0:N],
                             func=AF.Exp)
                else:
                    ev_a = I("act", "activation", wts, out=Ab[i % 3][0:nk, 0:N], in_=ARGb[i % 3][0:nk, 0:N],
                             func=AF.Exp, bias=flags[0:nk, 2 + u_["vis"]:3 + u_["vis"]])
                if u_["diag"]:
                    dn = min(128, N)
                    ev_a = I("dve", "tensor_tensor", [ev_a], out=Ab[i % 3][0:nk, 0:dn], in0=Ab[i % 3][0:nk, 0:dn],
                             in1=diagm[0:nk, 0:dn], op=ALU.mult)
                evA[i] = ev_a

            def stage3(i):
                u_ = units[i]
                nk, c0, N = u_["nk"], u_["c0"], u_["N"]
                ob = OB_[u_["chunk"] % 2]
                evAV[i] = I("pe", "matmul", [evA[i], o_free[u_["chunk"] % 2] if u_["first"] else None],
                            banks[ob][:, c0:c0 + N], lhsT=u_["v"], rhs=Ab[i % 3][0:nk, 0:N], start=u_["first"],
                            stop=u_["last"], skip_group_check=True)
                if u_["last"]:
                    qn, q0, hh, h = u_["qn"], u_["q0"], u_["hh"], u_["h"]
                    o_free[u_["chunk"] % 2] = I("dve", "tensor_tensor", [evAV[i], sg_ev], out=o_attn[:, h, q0:q0 + qn],
                                                in0=banks[ob][:, 0:qn], in1=SGc[:, hh, q0:q0 + qn], op=ALU.mult)

            nF = len(filler) if filler else 0
            fpos = [int((k + 0.5) * nU / nF) for k in range(nF)]
            for it in range(nU + 4):
                if 0 <= it - 4 < nU:
                    stage3(it - 4)
                if 0 <= it - 3 < nU:
                    stage2b(it - 3)
                if 0 <= it - 2 < nU:
                    stage2(it - 2)
                if 0 <= it - 1 < nU:
                    stage1(it - 1)
                if it < nU:
                    stage0(it)
                while filler and fpos and fpos[0] <= it:
                    fpos.pop(0)
                    filler.pop(0)()
            while filler:
                filler.pop(0)()

        for l in range(depth):
            emit_layer(l)

        yfree = [None, None]
        norm_done = {}
        fsrc = xT0 if DEBUG_STOP else xres
        for ci, (t0, t1) in enumerate(NCH[1:]):
            n = t1 - t0
            par = ci % 2
            ld = P.dma("sp", xst[par][:, :, 0:n], fsrc[:, :, t0:t1].rearrange("c p t -> p c t"), s_x[par],
                       waits=[norm_done.get(ci - 2)])
            r2 = rmsnorm_stats(xst[par], n, par, ld)
            for c in range(KC):
                yb = c % 2
                y_ev = I("dve", "scalar_tensor_tensor", [r2, ld, yfree[yb]], out=yst[:, yb, 0:n],
                         in0=xst[par][:, c, 0:n], scalar=gains[:, NL * 16 + c:NL * 16 + c + 1],
                         in1=rstd[par][:, 0:n], op0=ALU.mult, op1=ALU.mult)
                norm_done[ci] = y_ev
                yfree[yb] = P.dma("sp", outT[c, :, t0 - 16:t1 - 16], yst[:, yb, 0:n], s_out[yb], waits=[y_ev])
        fin = [(so.h, so.cnt) for so in s_out]

        with nc.Block() as block:
            @block.tensor
            def _(e):
                P.replay("pe", e)

            @block.scalar
            def _(e):
                P.replay("act", e)

            @block.vector
            def _(e):
                P.replay("dve", e)

            @block.gpsimd
            def _(e):
                P.replay("pool", e)

            @block.sync
            def _(e):
                P.replay("sp", e)
                for fh, fv in fin:
                    e.wait_ge(fh, fv)
    return nc


def _prep_inputs(x, meta_tokens, norm_gain, w_in, pool_w, pool_scale, w_attn_up, w_pool_up, w_out, final_gain, depth=NL):
    f32 = np.float32
    x = np.asarray(x, f32)
    B = x.shape[0]

    def regroup(w, kcn, ngrp):
        w = np.asarray(w, f32).reshape(NL, kcn, 128, ngrp, GW)
        return np.ascontiguousarray(w.transpose(0, 3, 2, 1, 4)).reshape(NL * ngrp, 128, kcn * GW)

    wir = regroup(w_in, 16, 40).reshape(NL, 40, 128, 16 * GW)
    shared = {f"w_in_r{l}": wir[l] for l in range(depth)}
    shared.update({
        "w_au_r": regroup(w_attn_up, 8, 8)[:depth * 8],
        "w_pu_r": regroup(w_pool_up, 8, 8)[:depth * 8],
        "w_out_r": regroup(w_out, 16, 8)[:depth * 8],
    })
    pw = np.asarray(pool_w, f32).reshape(NL, 4, 2, 128, 256)
    shared["pool_w_r"] = np.ascontiguousarray(pw.transpose(0, 3, 1, 2, 4)).reshape(NL, 128, 4 * 2 * 256)[:depth]
    g = np.concatenate([np.asarray(norm_gain, f32).reshape(NL, 16, 128), np.asarray(final_gain, f32).reshape(1, 16, 128)], 0)
    shared["gains"] = np.ascontiguousarray(g.transpose(2, 0, 1)).reshape(128, NL * 16 + 16)
    ps = np.asarray(pool_scale, f32).reshape(NL, 8, 128)
    shared["pscale"] = np.ascontiguousarray(ps.transpose(2, 0, 1)).reshape(128, NL * 8)
    shared["ones_f32"] = np.ones((128, 128), f32)
    j = np.arange(128)
    shared["negtri"] = -(j[:, None] >= j[None, :]).astype(f32)
    shared["negones"] = -np.ones((128, 128), f32)
    shared["diagmask"] = (j[:, None] < j[None, :]).astype(f32)

    if DEBUG_TINY:
        for k in list(shared):
            if k.startswith("w_"):
                shared[k] = np.ascontiguousarray(shared[k][:1])
    in_maps = []
    for r in range(N_CORES):
        b, half = r // 2, r % 2
        tok = np.zeros((T, D), f32)
        if half == 0:
            tok[0:PRE] = np.asarray(meta_tokens, f32)
        tok[PRE:] = x[b, half * RT:(half + 1) * RT]
        xT0 = np.ascontiguousarray(tok.T).reshape(16, 128, T)
        fl = np.zeros((128, 8), f32)
        if half == 0:
            fl[:, 0] = 1.0; fl[:, 1] = 0.0; fl[:, 2] = 0.0; fl[:, 3] = NEG; fl[:, 4] = 1.0; fl[:, 5] = 0.0
        else:
            fl[:, 0] = 0.0; fl[:, 1] = 1.0; fl[:, 2] = NEG; fl[:, 3] = 0.0; fl[:, 4] = 0.0; fl[:, 5] = 1.0
        fl[:, 6] = 1.0
        fl[:, 7] = EPS
        ic = np.zeros((128, 4, 16), f32)
        for gi, w in enumerate(POOL_W):
            if half == 0:
                ic[:, gi, :] = 1.0 / np.minimum(np.arange(16) + 1, w)
            else:
                ic[:, gi, :] = 1.0 / w
        m = dict(shared)
        m["xT0"] = xT0
        m["flags"] = fl
        m["invcnt"] = ic.reshape(128, 64)
        in_maps.append(m)
    return in_maps


_NC_CACHE = {}


def run(inputs, depth=NL):
    if depth not in _NC_CACHE:
        _NC_CACHE[depth] = build(depth)
    nc = _NC_CACHE[depth]
    in_maps = _prep_inputs(depth=depth, **inputs)
    res = run_bass_kernel_spmd(nc, in_maps, core_ids=list(range(N_CORES)))
    B = inputs["x"].shape[0]
    out = np.zeros((B, 2 * RT, D), np.float32)
    for r in range(N_CORES):
        b, half = r // 2, r % 2
        o = np.asarray(res.results[r]["outT"]).reshape(D, RT)
        out[b, half * RT:(half + 1) * RT, :] = o.T
    return out


def kernel(x, meta_tokens, norm_gain, w_in, pool_w, pool_scale, w_attn_up, w_pool_up, w_out, final_gain):
    return run(dict(x=x, meta_tokens=meta_tokens, norm_gain=norm_gain, w_in=w_in, pool_w=pool_w,
                    pool_scale=pool_scale, w_attn_up=w_attn_up, w_pool_up=w_pool_up, w_out=w_out,
                    final_gain=final_gain))
```

```python
import numpy as np
from contextlib import ExitStack
import concourse.bass as bass
import concourse.mybir as mybir
from concourse.bass_utils import run_bass_kernel_spmd

F32 = mybir.dt.float32
BF16 = mybir.dt.bfloat16
F32R = mybir.dt.float32r
AF = mybir.ActivationFunctionType
ALU = mybir.AluOpType

D = 2048
KC = 16
NL = 4
H = 8
DH = 128
PRE = 16
RT = 1024
T = PRE + RT
NB = RT // 128
GW = 256
PCH = [(0, 347), (347, 694), (694, 1040)]
NCH = [(0, 16)] + [(16 + 256 * i, 16 + 256 * (i + 1)) for i in range(4)]
POOL_W = (2, 4, 8, 16)
EPS = 1e-6
NEG = -30000.0
N_CORES = 8
NWBUF = 3
DEBUG_STOP = None
DEBUG_TINY = False


class DSem:
    def __init__(self, h):
        self.h = h
        self.cnt = 0


class Prog:
    ENG = ("pe", "act", "dve", "pool", "sp")

    def __init__(self, nc, st):
        self.nc = nc
        self.st = st
        self.q = {k: [] for k in self.ENG}
        self.sem = {k: st.enter_context(nc.semaphore("sem_" + k)) for k in self.ENG}
        self.cnt = {k: 0 for k in self.ENG}

    def dsem(self, name):
        return DSem(self.st.enter_context(self.nc.semaphore(name)))

    def I(self, eng, name, waits, *args, sig=True, **kw):
        ev = None
        if sig:
            self.cnt[eng] += 1
            ev = (self.sem[eng], self.cnt[eng])
        self.q[eng].append((name, args, kw, [w for w in waits if w is not None], ev, 1))
        return ev

    def dma(self, eng, out, in_, sem, waits=(), **kw):
        sem.cnt += 16
        ev = (sem.h, sem.cnt)
        kw = dict(kw)
        kw.update(out=out, in_=in_)
        self.q[eng].append(("dma_start", (), kw, [w for w in waits if w is not None], ev, 16))
        return ev

    def cc(self, sem, waits, **kw):
        sem.cnt += 1
        ev = (sem.h, sem.cnt)
        self.q["pool"].append(("collective_compute", (), kw, [w for w in waits if w is not None], ev, 1))
        return ev

    def last(self, eng):
        return (self.sem[eng], self.cnt[eng]) if self.cnt[eng] > 0 else None

    def barrier(self, engs=("pe", "act", "dve", "sp"), extra=()):
        evs = [self.last(o) for o in engs] + list(extra)
        for e in engs:
            self.q[e].append((None, (), {}, [w for w in evs if w is not None and w[0] is not self.sem[e]], None, 0))

    def replay(self, eng, e):
        waited = {}
        for name, args, kw, waits, ev, inc in self.q[eng]:
            for (h, v) in waits:
                k = h.num
                if waited.get(k, 0) >= v:
                    continue
                e.wait_ge(h, v)
                waited[k] = v
            if name is None:
                continue
            ins = getattr(e, name)(*args, **kw)
            if ev is not None:
                ins.then_inc(ev[0], inc)


def build(depth=NL):
    nc = bass.Bass("TRN2", target_bir_lowering=False)

    def din(name, shape, dt=F32):
        return nc.dram_tensor(name, shape, dt, kind="ExternalInput")

    xT0 = din("xT0", [16, 128, T])
    NG = 1 if DEBUG_TINY else 40
    w_in_l = [din(f"w_in_r{l}", [NG, 128, 16 * GW]) for l in range(depth)]
    NG8 = 1 if DEBUG_TINY else depth * 8
    w_au_r = din("w_au_r", [NG8, 128, 8 * GW])
    w_pu_r = din("w_pu_r", [NG8, 128, 8 * GW])
    w_out_r = din("w_out_r", [NG8, 128, 16 * GW])
    pool_w_r = din("pool_w_r", [depth, 128, 4 * 2 * 256])
    gains_d = din("gains", [128, NL * 16 + 16])
    pscale_d = din("pscale", [128, NL * 8])
    flags_d = din("flags", [128, 8])
    invcnt_d = din("invcnt", [128, 4 * 16])
    ones_d = din("ones_f32", [128, 128])
    negtri_d = din("negtri", [128, 128])
    negones_d = din("negones", [128, 128])
    diag_d = din("diagmask", [128, 128])
    outT = nc.dram_tensor("outT", [16, 128, RT], F32, kind="ExternalOutput")

    xres = nc.dram_tensor("xres", [16, 128, T], F32)
    sendK = [[nc.dram_tensor(f"sendK{l}_{a}", [4 * 128, T], BF16) for a in range(2)] for l in range(depth)]
    recvK = [[nc.dram_tensor(f"recvK{l}_{a}", [2 * 4 * 128, T], BF16) for a in range(2)] for l in range(depth)]
    sendV = [[nc.dram_tensor(f"sendV{l}_{a}", [9 * 128, 512], BF16) for a in range(2)] for l in range(depth)]
    recvV = [[nc.dram_tensor(f"recvV{l}_{a}", [2 * 9 * 128, 512], BF16) for a in range(2)] for l in range(depth)]
    sendU = [nc.dram_tensor(f"sendU{l}", [128, 128], F32) for l in range(depth)]
    recvU = [nc.dram_tensor(f"recvU{l}", [256, 128], F32) for l in range(depth)]
    gate_d = [[nc.dram_tensor(f"gate{l}_{k}", [16, 128, T], F32) for k in range(2)] for l in range(depth)]
    GROUPS = [[0, 1], [2, 3], [4, 5], [6, 7]]

    with ExitStack() as st:
        def sb(name, shape, dt):
            return st.enter_context(nc.sbuf_tensor(name, shape, dt))

        P = Prog(nc, st)
        I = P.I

        hT = sb("hT", [128, KC, T], BF16)
        RA = sb("RA", [128, 8768], F32)
        xst_pre = sb("xst_pre", [128, KC, 16], F32)
        RB = sb("RB", [128, 12928], F32)
        o_pool = sb("o_pool", [128, 8, T], BF16)
        wbufs = [sb(f"wbuf{i}", [128, 16 * GW], BF16) for i in range(NWBUF)]
        pool_w_sb = sb("pool_w_sb", [128, 4 * 2 * 256], BF16)
        RC = sb("RC", [128, 6240], F32)
        RD = sb("RD", [128, 4384], F32)
        gains = sb("gains_sb", [128, NL * 16 + 16], F32)
        pscale = sb("pscale_sb", [128, NL * 8], F32)
        flags = sb("flags_sb", [128, 8], F32)
        invcnt = sb("invcnt_sb", [128, 4, 16], F32)
        ones_f = sb("ones_sb", [128, 128], F32)
        negtri = sb("negtri_sb", [128, 128], BF16)
        negones = sb("negones_sb", [128, 128], BF16)
        diagm = sb("diag_sb", [128, 128], BF16)

        def view(region, boff, dt, shape):
            esz = 4 if dt == F32 else 2
            n = int(np.prod(shape[1:])) * esz
            ap = region[:, boff // 4:(boff + n) // 4]
            if dt != F32:
                ap = ap.bitcast(dt)
            if len(shape) == 3:
                ap = ap.rearrange("p (a b) -> p a b", a=shape[1])
            return ap

        xst = [view(RA, i * 16384, F32, [128, KC, 256]) for i in range(2)]
        KT_own = view(RA, 0, BF16, [128, H, T])
        V_own = view(RA, 16640, BF16, [128, 9, 1024])
        merged = view(RA, 0, BF16, [128, KC, T])
        UBUF = view(RB, 0, F32, [128, 8, 16 + T])
        o_attn = view(RB, 0, BF16, [128, H, T])
        ASC = 16640
        Ebuf = view(RB, ASC, F32, [128, 512])
        ARGb = [view(RB, ASC + 2048 + i * 2048, F32, [128, 512]) for i in range(3)]
        Rb = [view(RB, ASC + 8192 + i * 2048, F32, [128, 512]) for i in range(2)]
        SPb = [view(RB, ASC + 12288 + i * 1024, BF16, [128, 512]) for i in range(3)]
        Ab = [view(RB, ASC + 15360 + i * 1024, BF16, [128, 512]) for i in range(3)]
        QT2 = [view(RB, ASC + 18432 + i * 8320, BF16, [128, 2, T]) for i in range(2)]
        SGA2 = [view(RB, ASC + 18432 + 4160 + i * 8320, BF16, [128, 2, T]) for i in range(2)]
        PA = view(RC, 0, F32, [128, 16 + T])
        PBb = view(RC, 4224, F32, [128, 16 + T])
        pooledbf = view(RC, 8448, BF16, [128, 2, T])
        sgp2 = [view(RC, 12608, BF16, [128, 2, T]), view(RC, 17344, BF16, [128, 2, T])]
        uhalo = view(RC, 16768, F32, [128, 8, 16])
        tmp16 = view(RC, 17280, F32, [128, 16])
        SGA = [view(RC, i * 4160, F32, [128, T]) for i in range(2)]
        SGP = [view(RC, 8320 + i * 4160, F32, [128, T]) for i in range(2)]
        T1 = [view(RC, 16640 + i * 4160, F32, [128, T]) for i in range(2)]
        GST = [view(RC, i * 4160, F32, [128, T]) for i in range(2)]
        yst = view(RC, 0, F32, [128, 2, 512])
        KT_par = view(RD, 0, BF16, [128, 4, T])
        V_par = view(RD, 8320, BF16, [128, 4, 9 * 128])
        xc = view(RD, 0, F32, [128, 2, T])
        sq = view(RD, 0, F32, [128, 2, 512])
        sqb = view(RD, 0, BF16, [128, 2, 512])
        rs_tmp = [view(RD, 4096 + i * 1024, F32, [128, 256]) for i in range(2)]
        rstd = [view(RD, 6144 + i * 1024, F32, [128, 256]) for i in range(2)]

        ps_all = st.enter_context(nc.psum_tensor("ps_all", [128, 4096], F32))
        banks = [ps_all[:, 512 * i:512 * (i + 1)] for i in range(8)]
        bank_free = [None] * 8

        s_const = P.dsem("s_const")
        s_w = [P.dsem(f"s_w{i}") for i in range(NWBUF)]
        s_pw = P.dsem("s_pw")
        s_x = [P.dsem(f"s_x{i}") for i in range(3)]
        s_xc = [P.dsem(f"s_xc{i}") for i in range(2)]
        s_xs = [P.dsem(f"s_xs{i}") for i in range(2)]
        s_send = [P.dsem(f"s_send{i}") for i in range(5)]
        s_cck = [P.dsem(f"s_cck{i}") for i in range(2)]
        s_ccv = [P.dsem(f"s_ccv{i}") for i in range(2)]
        s_par = [P.dsem(f"s_par{i}") for i in range(4)]
        s_gst = [P.dsem(f"s_gst{i}") for i in range(2)]
        s_gld = [P.dsem(f"s_gld{i}") for i in range(4)]
        s_hal = P.dsem("s_hal")
        s_out = [P.dsem(f"s_out{i}") for i in range(2)]
        s_cc = [P.dsem(f"s_cc{i}") for i in range(3)]

        for dst, srcd in ((gains, gains_d), (pscale, pscale_d), (flags, flags_d), (ones_f, ones_d)):
            P.dma("pool", dst[:, :], srcd[:, :], s_const)
        P.dma("pool", invcnt[:, :, :], invcnt_d.ap().rearrange("p (a b) -> p a b", a=4), s_const)
        for dst, srcd in ((negtri, negtri_d), (negones, negones_d), (diagm, diag_d)):
            P.dma("pool", dst[:, :], srcd[:, :], s_const)
        const_ev = (s_const.h, s_const.cnt)
        eps_col = flags[:, 7:8]
        I("dve", "memset", [], RC[:, :], 0.0)

        wlist = []
        for l in range(depth):
            def wi(g, l=l):
                return (("in", l, g), w_in_l[l][0 if DEBUG_TINY else g], 16 * GW)
            seq = []
            seq += [wi(16 + g) for g in range(4)]
            seq += [wi(4 + g) for g in range(4)]
            for g in range(4):
                seq += [wi(20 + g), wi(8 + g)]
            seq += [wi(0), wi(12)]
            for hp in range(4):
                if hp < 3:
                    seq += [wi(hp + 1), wi(12 + hp + 1)]
                for mg in (2 * hp, 2 * hp + 1):
                    seq += [wi(24 + mg), wi(32 + mg)]
            for mg in range(8):
                seq += [(("au", l, mg), w_au_r[0 if DEBUG_TINY else l * 8 + mg], 8 * GW),
                        (("pu", l, mg), w_pu_r[0 if DEBUG_TINY else l * 8 + mg], 8 * GW)]
            for og in range(8):
                seq += [(("out", l, og), w_out_r[0 if DEBUG_TINY else l * 8 + og], 16 * GW)]
            wlist += seq
        wstate = {"issued": 0, "next": 0, "free": {}, "ev": {}}

        def w_try_issue():
            while wstate["issued"] < len(wlist):
                i = wstate["issued"]
                if i >= NWBUF and (i - NWBUF) not in wstate["free"]:
                    break
                b = i % NWBUF
                _, srcap, n = wlist[i]
                fr = wstate["free"].get(i - NWBUF)
                wstate["ev"][i] = P.dma("pool", wbufs[b][:, 0:n], srcap, s_w[b], waits=[fr],
                                         max_dma_last_dim=8192)
                wstate["issued"] += 1

        def w_next(kcn, key=None):
            i = wstate["next"]
            wstate["next"] += 1
            if key is not None:
                assert wlist[i][0] == key, (wlist[i][0], key)
            w_try_issue()
            assert i in wstate["ev"], "weight load not issued (too many live groups)"
            b = i % NWBUF
            W = wbufs[b][:, 0:kcn * GW].rearrange("p (k j) -> p k j", k=kcn)
            return W, wstate["ev"][i], i

        def w_release(i, ev):
            wstate["free"][i] = ev
            w_try_issue()

        pbank_rr = {"i": 0}

        def next_bank(lo, hi):
            b = lo + pbank_rr["i"] % (hi - lo)
            pbank_rr["i"] += 1
            return b

        def proj_fm(W, wev, blk, kcn, rhs_fn, evac_fn, extra_waits=(), chunk_waits=None):
            pe_ev = None
            for ci, (t0, t1) in enumerate(PCH):
                n = t1 - t0
                b = next_bank(0, 6)
                for kc in range(kcn):
                    waits = []
                    if kc == 0:
                        waits = [wev, bank_free[b]] + list(extra_waits)
                        if chunk_waits:
                            waits += list(chunk_waits[ci])
                    pe_ev = I("pe", "matmul", waits, banks[b][:, 0:n], lhsT=W[:, kc, blk * 128:(blk + 1) * 128],
                              rhs=rhs_fn(kc, t0, t1), start=(kc == 0), stop=(kc == kcn - 1), sig=(kc == kcn - 1))
                bank_free[b] = evac_fn(banks[b][:, 0:n], ci, t0, t1, pe_ev)
            return pe_ev

        def hT_rhs(kc, t0, t1):
            return hT[:, kc, t0:t1]

        def evac_copy(dst_fn, scale=None, extra=(), eng=None):
            def f(ps, ci, t0, t1, pe_ev):
                w = [pe_ev] + list(extra)
                if (ci % 2 == 0 and eng is None) or eng == "dve":
                    if scale is None:
                        return I("dve", "tensor_copy", w, out=dst_fn(t0, t1), in_=ps)
                    return I("dve", "tensor_scalar", w, out=dst_fn(t0, t1), in0=ps, scalar1=scale,
                             scalar2=None, op0=ALU.mult)
                return I("act", "activation", w, out=dst_fn(t0, t1), in_=ps, func=AF.Copy,
                         scale=(1.0 if scale is None else scale))
            return f

        def evac_act(dst_fn, func, extra_fn=None):
            def f(ps, ci, t0, t1, pe_ev):
                w = [pe_ev] + (list(extra_fn()) if extra_fn else [])
                return I("act", "activation", w, out=dst_fn(t0, t1), in_=ps, func=func)
            return f

        def rmsnorm_stats(stg, n, par, ld):
            b = 6 + par
            sq_free = [None, None]
            pe_ev = None
            for c in range(KC):
                a_ev = I("act", "activation", [ld, sq_free[c % 2], const_ev], out=sqb[:, c % 2, 0:n],
                         in_=stg[:, c, 0:n], func=AF.Square)
                pe_ev = I("pe", "matmul", [a_ev, bank_free[b] if c == 0 else None], banks[b][:, 0:n],
                          lhsT=negones[:, :], rhs=sqb[:, c % 2, 0:n],
                          start=(c == 0), stop=(c == KC - 1))
                sq_free[c % 2] = pe_ev
            r1 = I("act", "activation", [pe_ev], out=rs_tmp[par][:, 0:n], in_=banks[b][:, 0:n], func=AF.Sqrt,
                   bias=eps_col, scale=-1.0 / D)
            r2 = I("dve", "reciprocal", [r1], out=rstd[par][:, 0:n], in_=rs_tmp[par][:, 0:n])
            bank_free[b] = r1
            return r2

        def emit_layer(l):
            src = xT0 if l == 0 else xres
            P.barrier()
            pw_ev = P.dma("pool", pool_w_sb[:, :], pool_w_r[l], s_pw, waits=[P.last("pe")],
                          max_dma_last_dim=8192)

            norm_done = {}
            for ci, (t0, t1) in enumerate(NCH):
                n = t1 - t0
                par = ci % 2
                stg = xst_pre if ci == 0 else xst[par]
                ld = P.dma("sp", stg[:, :, 0:n], src[:, :, t0:t1].rearrange("c p t -> p c t"),
                           s_x[2 if ci == 0 else par], waits=[norm_done.get(ci - 2)])
                r2 = rmsnorm_stats(stg, n, par, ld)
                for c in range(KC):
                    norm_done[ci] = I("dve", "scalar_tensor_tensor", [r2, ld], out=hT[:, c, t0:t1],
                                      in0=stg[:, c, 0:n], scalar=gains[:, l * 16 + c:l * 16 + c + 1],
                                      in1=rstd[par][:, 0:n], op0=ALU.mult, op1=ALU.mult)
            def _cov(t0, t1):
                return [norm_done[ci] for ci, (a, b) in enumerate(NCH) if a < t1 and b > t0]
            hT_chunk_ready = [_cov(t0, t1) for (t0, t1) in PCH]
            normA_done = norm_done[len(NCH) - 1]
            if DEBUG_STOP == "A":
                P.barrier()
                return

            z_ev = I("dve", "memset", [], UBUF[:, :, 0:16], 0.0)
            for g in range(4):
                W, wev, wi_ = w_next(16)
                for blk in range(2):
                    ub = 2 * g + blk
                    pe_ev = proj_fm(W, wev, blk, 16, hT_rhs,
                                    evac_copy(lambda t0, t1, ub=ub: UBUF[:, ub, 16 + t0:16 + t1]),
                                    chunk_waits=(hT_chunk_ready if (g == 0 and blk == 0) else None))
                w_release(wi_, pe_ev)
            u_done = [P.last("dve"), P.last("act")]
            if DEBUG_STOP == "Ba":
                P.barrier()
                return
            su = P.dma("sp", sendU[l].ap().rearrange("p (a b) -> p a b", a=8), UBUF[:, :, T:T + 16],
                       s_send[0], waits=u_done)
            cc_u_ev = P.cc(s_cc[0], [su], kind="AllGather", op=ALU.bypass, replica_groups=GROUPS,
                           ins=[sendU[l].ap().opt()], outs=[recvU[l].ap().opt()])
            if DEBUG_STOP == "Bb":
                P.barrier(extra=[cc_u_ev])
                return

            hl = P.dma("sp", uhalo[:, :, :], recvU[l][0:128, :].rearrange("p (a b) -> p a b", a=8), s_hal,
                       waits=[cc_u_ev])
            h1 = I("dve", "tensor_scalar", [hl, const_ev], out=uhalo[:, :, :], in0=uhalo[:, :, :],
                   scalar1=flags[:, 5:6], scalar2=None, op0=ALU.mult)
            h2 = I("dve", "scalar_tensor_tensor", [h1, z_ev] + u_done, out=UBUF[:, :, 16:32], in0=UBUF[:, :, 16:32],
                   scalar=flags[:, 4:5], in1=uhalo[:, :, :], op0=ALU.mult, op1=ALU.add)

            for g in range(4):
                W, wev, wi_ = w_next(16, ("in", l, 4 + g))
                for blk in range(2):
                    h = 2 * g + blk
                    pe_ev = proj_fm(W, wev, blk, 16, hT_rhs,
                                    evac_copy(lambda t0, t1, h=h: KT_own[:, h, t0:t1], eng="act",
                                              extra=[normA_done]))
                w_release(wi_, pe_ev)
            k_done = [P.last("act")]
            cc_k_ev = []
            for a in range(2):
                sk = P.dma("sp", sendK[l][a].ap().rearrange("(h p) t -> p h t", p=128),
                           KT_own[:, 4 * a:4 * a + 4, :], s_send[1 + a], waits=k_done)
                cc_k_ev.append(P.cc(s_cck[a], [sk], kind="AllGather", op=ALU.bypass, replica_groups=GROUPS,
                                    ins=[sendK[l][a].ap().opt()], outs=[recvK[l][a].ap().opt()]))

            tblocks = [(0, 16)] + [(16 + 128 * i, 16 + 128 * (i + 1)) for i in range(NB)]
            opool_done = {}
            cc_v_ev = []
            for pg in range(4):
                w = POOL_W[pg]
                sgp = sgp2[pg % 2]
                for j in range(2):
                    ub = 2 * pg + j
                    u = UBUF[:, ub, :]
                    cur = u
                    d = 1
                    tgl = 0
                    ev = h2
                    while d < w:
                        dst = PA if tgl == 0 else PBb
                        ev = I("dve", "tensor_tensor", [ev], out=dst[:, d:16 + T], in0=cur[:, d:16 + T],
                               in1=cur[:, 0:16 + T - d], op=ALU.add)
                        cur = dst
                        d *= 2
                        tgl ^= 1
                    I("dve", "scalar_tensor_tensor", [ev], out=pooledbf[:, j, 16:T], in0=cur[:, 32:16 + T],
                      scalar=1.0 / w, in1=u[:, 32:16 + T], op0=ALU.mult, op1=ALU.subtract)
                    t16 = I("dve", "tensor_tensor", [ev, const_ev], out=tmp16[:, :], in0=cur[:, 16:32],
                            in1=invcnt[:, pg, :], op=ALU.mult)
                    I("dve", "tensor_tensor", [t16], out=pooledbf[:, j, 0:16], in0=tmp16[:, :], in1=u[:, 16:32],
                      op=ALU.subtract)
                pl_ev = P.last("dve")
                W, wev, wi_ = w_next(16, ("in", l, 20 + pg))
                for blk in range(2):
                    pe_ev = proj_fm(W, wev, blk, 16, hT_rhs,
                                    evac_act(lambda t0, t1, blk=blk, sgp=sgp: sgp[:, blk, t0:t1], AF.Silu,
                                             extra_fn=lambda pg=pg: [opool_done.get(pg - 2)]))
                w_release(wi_, pe_ev)
                sgp_ev = P.last("act")
                W, wev, wi_ = w_next(16, ("in", l, 8 + pg))
                pe_ev = None
                for tb, (t0, t1) in enumerate(tblocks):
                    m = t1 - t0
                    b = next_bank(0, 6)
                    for kc in range(KC):
                        pe_ev = I("pe", "matmul", ([wev, bank_free[b]] if kc == 0 else []), banks[b][0:m, 0:GW],
                                  lhsT=hT[:, kc, t0:t1], rhs=W[:, kc, :], start=(kc == 0), stop=(kc == KC - 1),
                                  sig=(kc == KC - 1))
                    bank_free[b] = I("act", "activation", [pe_ev], out=V_own[0:m, tb, pg * GW:(pg + 1) * GW],
                                     in_=banks[b][0:m, 0:GW], func=AF.Copy)
                w_release(wi_, pe_ev)
                if pg % 2 == 1:
                    a = pg // 2
                    sv = P.dma("sp", sendV[l][a].ap().rearrange("(b p) c -> p b c", p=128),
                               V_own[:, :, 512 * a:512 * (a + 1)], s_send[3 + a], waits=[P.last("act")])
                    cc_v_ev.append(P.cc(s_ccv[a], [sv], kind="AllGather", op=ALU.bypass, replica_groups=GROUPS,
                                        ins=[sendV[l][a].ap().opt()], outs=[recvV[l][a].ap().opt()]))
                for ob in range(2):
                    for ci, (t0, t1) in enumerate(PCH):
                        n = t1 - t0
                        b = next_bank(0, 6)
                        for kc in range(2):
                            o0 = (pg * 2 + kc) * 256 + ob * 128
                            pe_ev = I("pe", "matmul", ([pl_ev, pw_ev, bank_free[b]] if kc == 0 else []),
                                      banks[b][:, 0:n], lhsT=pool_w_sb[:, o0:o0 + 128], rhs=pooledbf[:, kc, t0:t1],
                                      start=(kc == 0), stop=(kc == 1), sig=(kc == 1))
                        cidx = l * 8 + 2 * pg + ob
                        bank_free[b] = I("dve", "scalar_tensor_tensor", [pe_ev, sgp_ev, const_ev],
                                         out=o_pool[:, 2 * pg + ob, t0:t1], in0=banks[b][:, 0:n],
                                         scalar=pscale[:, cidx:cidx + 1], in1=sgp[:, ob, t0:t1],
                                         op0=ALU.mult, op1=ALU.mult)
                opool_done[pg] = P.last("dve")
            P.barrier()
            if DEBUG_STOP == "B":
                return

            def qga_tasks(hp, bank_rng):
                tasks = []
                stt = {}

                def mk(kind, blk, ci):
                    def run():
                        if (kind, "W") not in stt:
                            stt[(kind, "W")] = w_next(16, ("in", l, (hp if kind == "q" else 12 + hp)))
                        W, wev, wi_ = stt[(kind, "W")]
                        t0, t1 = PCH[ci]
                        n = t1 - t0
                        b = next_bank(*bank_rng)
                        pe_ev = None
                        for kc in range(KC):
                            pe_ev = I("pe", "matmul", ([wev, bank_free[b]] if kc == 0 else []), banks[b][:, 0:n],
                                      lhsT=W[:, kc, blk * 128:(blk + 1) * 128], rhs=hT[:, kc, t0:t1],
                                      start=(kc == 0), stop=(kc == KC - 1), sig=(kc == KC - 1))
                        if kind == "q":
                            bank_free[b] = I("dve", "tensor_scalar", [pe_ev], out=QT2[hp % 2][:, blk, t0:t1],
                                             in0=banks[b][:, 0:n], scalar1=float(DH) ** -0.5, scalar2=None,
                                             op0=ALU.mult)
                        else:
                            bank_free[b] = I("dve", "tensor_copy", [pe_ev], out=SGA2[hp % 2][:, blk, t0:t1],
                                             in_=banks[b][:, 0:n])
                        if blk == 1 and ci == len(PCH) - 1:
                            w_release(wi_, pe_ev)
                    return run
                for kind in ("q", "g"):
                    for blk in range(2):
                        for ci in range(len(PCH)):
                            tasks.append(mk(kind, blk, ci))
                return tasks

            gst_state = {"free": [None, None], "n": 0, "store": {}}

            def gate_tasks(hp):
                tasks = []
                stt = {}

                def mk(kind, mg, blk, ci):
                    def run():
                        wk = (kind, mg)
                        if wk not in stt:
                            stt[wk] = w_next(16, ("in", l, (24 if kind == 0 else 32) + mg))
                        W, wev, wi_ = stt[wk]
                        t0, t1 = PCH[ci]
                        n = t1 - t0
                        if ci == 0:
                            stt["sidx"] = gst_state["n"] % 2
                            gst_state["n"] += 1
                            stt["evs"] = []
                        sidx = stt["sidx"]
                        b = 7
                        pe_ev = None
                        for kc in range(KC):
                            pe_ev = I("pe", "matmul", ([wev, bank_free[b]] if kc == 0 else []), banks[b][:, 0:n],
                                      lhsT=W[:, kc, blk * 128:(blk + 1) * 128], rhs=hT[:, kc, t0:t1],
                                      start=(kc == 0), stop=(kc == KC - 1), sig=(kc == KC - 1))
                        bank_free[b] = I("dve", "tensor_copy", [pe_ev, gst_state["free"][sidx] if ci == 0 else None],
                                         out=GST[sidx][:, t0:t1], in_=banks[b][:, 0:n])
                        stt["evs"].append(bank_free[b])
                        if ci == len(PCH) - 1:
                            c = 2 * mg + blk
                            ev = P.dma("sp", gate_d[l][kind][c], GST[sidx][:, :], s_gst[sidx], waits=stt["evs"])
                            gst_state["free"][sidx] = ev
                            gst_state["store"][(kind, c)] = ev
                            if blk == 1:
                                w_release(wi_, pe_ev)
                    return run
                for mg in (2 * hp, 2 * hp + 1):
                    for kind in (0, 1):
                        for blk in range(2):
                            for ci in range(len(PCH)):
                                tasks.append(mk(kind, mg, blk, ci))
                return tasks

            for t_ in qga_tasks(0, (0, 6)):
                t_()

            def par_loads(hp, free_ev):
                evs = []
                for hh in range(2):
                    h = 2 * hp + hh
                    slot = h % 4
                    ha, hl_ = h // 4, h % 4
                    P.dma("sp", KT_par[:, slot, :], recvK[l][ha][hl_ * 128:(hl_ + 1) * 128, :], s_par[slot],
                          waits=[cc_k_ev[ha], free_ev])
                    e2 = P.dma("sp", V_par[:, slot, :].rearrange("p (b c) -> p b c", b=9),
                               recvV[l][ha][0:9 * 128, hl_ * 128:(hl_ + 1) * 128].rearrange("(b p) c -> p b c", p=128),
                               s_par[slot], waits=[cc_v_ev[ha], free_ev])
                    evs.append(e2)
                return evs

            pair_done = {}
            par_evs = {0: par_loads(0, None)}
            for hp in range(4):
                if hp < 3:
                    par_evs[hp + 1] = par_loads(hp + 1, pair_done.get(hp - 1))
                P.barrier(engs=("pe", "act", "dve"))
                att_ctx["par"] = par_evs[hp]
                attention_pair(hp, filler=((qga_tasks(hp + 1, (7, 8)) if hp < 3 else []) + gate_tasks(hp)))
                pair_done[hp] = P.last("pe")
            P.barrier()
            if DEBUG_STOP == "C":
                return
            def load_gate(kind, mg, blk, free_ev):
                c = 2 * mg + blk
                buf = (SGA if kind == 0 else SGP)[blk]
                ld = P.dma("sp", buf[:, :], gate_d[l][kind][c], s_gld[kind * 2 + blk],
                           waits=[gst_state["store"][(kind, c)], free_ev])
                return I("act", "activation", [ld], out=buf[:, :], in_=buf[:, :], func=AF.Sigmoid)

            ga_ev = {}
            gp_ev = {}
            for blk in range(2):
                ga_ev[(0, blk)] = load_gate(0, 0, blk, gst_state["free"][blk])
                gp_ev[(0, blk)] = load_gate(1, 0, blk, None)
            for mg in range(8):
                W, wev, wi_ = w_next(8, ("au", l, mg))
                for blk in range(2):
                    def ev_ya(ps, ci, t0, t1, pe_ev, blk=blk):
                        return I("dve", "tensor_tensor", [pe_ev, ga_ev[(mg, blk)]], out=T1[blk][:, t0:t1],
                                 in0=ps, in1=SGA[blk][:, t0:t1], op=ALU.mult)
                    pe_ev = proj_fm(W, wev, blk, 8, lambda kc, t0, t1: o_attn[:, kc, t0:t1], ev_ya)
                    if mg < 7:
                        ga_ev[(mg + 1, blk)] = load_gate(0, mg + 1, blk, P.last("dve"))
                w_release(wi_, pe_ev)
                W, wev, wi_ = w_next(8, ("pu", l, mg))
                for blk in range(2):
                    c = 2 * mg + blk

                    def ev_yp(ps, ci, t0, t1, pe_ev, blk=blk, c=c):
                        e1 = I("dve", "tensor_tensor", [pe_ev, gp_ev[(mg, blk)]], out=SGP[blk][:, t0:t1], in0=ps,
                               in1=SGP[blk][:, t0:t1], op=ALU.mult)
                        return I("dve", "tensor_tensor", [e1], out=merged[:, c, t0:t1], in0=T1[blk][:, t0:t1],
                                 in1=SGP[blk][:, t0:t1], op=ALU.add)
                    pe_ev = proj_fm(W, wev, blk, 8, lambda kc, t0, t1: o_pool[:, kc, t0:t1], ev_yp)
                    if mg < 7:
                        gp_ev[(mg + 1, blk)] = load_gate(1, mg + 1, blk, P.last("dve"))
                w_release(wi_, pe_ev)
            P.barrier()
            if DEBUG_STOP == "D":
                return

            xs_ev = [None, None]
            for og in range(8):
                W, wev, wi_ = w_next(16)
                for blk in range(2):
                    c = 2 * og + blk
                    xb = c % 2
                    ld = P.dma("sp", xc[:, xb, :], src[c], s_xc[xb], waits=[xs_ev[xb]])

                    def ev_res(ps, ci, t0, t1, pe_ev, xb=xb, ld=ld):
                        return I("dve", "tensor_tensor", [pe_ev, ld], out=xc[:, xb, t0:t1], in0=ps,
                                 in1=xc[:, xb, t0:t1], op=ALU.add)
                    pe_ev = proj_fm(W, wev, blk, 16, lambda kc, t0, t1: merged[:, kc, t0:t1], ev_res)
                    xs_ev[xb] = P.dma("sp", xres[c], xc[:, xb, :], s_xs[xb], waits=[P.last("dve")])
                w_release(wi_, pe_ev)
            P.barrier(extra=[xs_ev[0], xs_ev[1]])

        att_ctx = {}

        def attention_pair(hp, filler=None):
            QTc = QT2[hp % 2]
            SGc = SGA2[hp % 2]
            sg_ev = I("act", "activation", [], out=SGc[:, :, :], in_=SGc[:, :, :], func=AF.Silu)
            units = []
            chunk_id = 0
            for hh in range(2):
                h = 2 * hp + hh

                def own_blk(kb):
                    return (KT_own[:, h, 16 + 128 * kb:16 + 128 * (kb + 1)],
                            V_own[:, 1 + kb, h * 128:(h + 1) * 128], 128)

                def own_pre():
                    return (KT_own[:, h, 0:16], V_own[0:16, 0, h * 128:(h + 1) * 128], 16)

                slot = h % 4

                def par_blk(kb):
                    return (KT_par[:, slot, 16 + 128 * kb:16 + 128 * (kb + 1)],
                            V_par[:, slot, (1 + kb) * 128:(2 + kb) * 128], 128)

                def par_pre():
                    return (KT_par[:, slot, 0:16], V_par[0:16, slot, 0:128], 16)

                for c in range(2):
                    q0 = 16 + 512 * c
                    ul = []
                    for i in (3, 2, 1, 0):
                        kt, v, nk = own_blk(4 * c + i)
                        ul.append(dict(kt=kt, v=v, nk=nk, c0=128 * i, N=512 - 128 * i, vis=None, diag=True))
                    for kb in range(4 * c - 1, -1, -1):
                        kt, v, nk = own_blk(kb)
                        ul.append(dict(kt=kt, v=v, nk=nk, c0=0, N=512, vis=None, diag=False))
                    kt, v, nk = own_pre()
                    ul.append(dict(kt=kt, v=v, nk=nk, c0=0, N=512, vis=0, diag=False))
                    for kb in range(NB - 1, -1, -1):
                        kt, v, nk = par_blk(kb)
                        ul.append(dict(kt=kt, v=v, nk=nk, c0=0, N=512, vis=1, diag=False, pw=att_ctx["par"][hh]))
                    kt, v, nk = par_pre()
                    ul.append(dict(kt=kt, v=v, nk=nk, c0=0, N=512, vis=1, diag=False, pw=att_ctx["par"][hh]))
                    for k, u_ in enumerate(ul):
                        u_.update(q0=q0, hh=hh, h=h, first=(k == 0), last=(k == len(ul) - 1), chunk=chunk_id, qn=512)
                    units += ul
                    chunk_id += 1
                kt, v, nk = own_pre()
                units.append(dict(kt=kt, v=v, nk=nk, c0=0, N=16, vis=None, diag=True, q0=0, hh=hh, h=h,
                                  first=True, last=True, chunk=chunk_id, qn=16))
                chunk_id += 1

            nU = len(units)
            evE = [None] * nU
            evSP = [None] * nU
            evY = [None] * nU
            evARG = [None] * nU
            evR = [None] * nU
            evA = [None] * nU
            evAV = [None] * nU
            o_free = {0: None, 1: None}
            rz_ev = {}
            SB_ = (0, 1, 2)
            YB_ = (3, 4)
            OB_ = (5, 6)

            def stage0(i):
                u_ = units[i]
                nk, c0, N = u_["nk"], u_["c0"], u_["N"]
                sbk = SB_[i % 3]
                qap = QTc[:, u_["hh"], u_["q0"] + c0:u_["q0"] + c0 + N]
                ev_s = I("pe", "matmul", [evARG[i - 3] if i >= 3 else None, u_.get("pw")], banks[sbk][0:nk, c0:c0 + N],
                         lhsT=u_["kt"], rhs=qap, start=True, stop=False, skip_group_check=True)
                evE[i] = I("act", "activation", [ev_s], out=Ebuf[0:nk, 0:N], in_=banks[sbk][0:nk, c0:c0 + N],
                           func=AF.Exp)

            def stage1(i):
                u_ = units[i]
                nk, c0, N = u_["nk"], u_["c0"], u_["N"]
                sc = 1.0 if u_["vis"] is None else flags[0:nk, u_["vis"]:u_["vis"] + 1]
                ev_sp = I("act", "activation", [evE[i], evY[i - 3] if i >= 3 else None, const_ev],
                          out=SPb[i % 3][0:nk, 0:N], in_=Ebuf[0:nk, 0:N], func=AF.Ln,
                          bias=1.0, scale=sc)
                if u_["diag"]:
                    dn = min(128, N)
                    ev_sp = I("dve", "tensor_tensor", [ev_sp, const_ev], out=SPb[i % 3][0:nk, 0:dn],
                              in0=SPb[i % 3][0:nk, 0:dn], in1=diagm[0:nk, 0:dn], op=ALU.mult)
                evSP[i] = ev_sp

            def stage2(i):
                u_ = units[i]
                nk, c0, N = u_["nk"], u_["c0"], u_["N"]
                sbk = SB_[i % 3]
                ybk = YB_[i % 2]
                rb = Rb[u_["chunk"] % 2]
                I("pe", "matmul", [evSP[i], const_ev], banks[sbk][0:nk, c0:c0 + N], lhsT=negtri[0:nk, 0:nk],
                  rhs=SPb[i % 3][0:nk, 0:N], start=False, stop=True, skip_group_check=True, sig=False)
                evY[i] = I("pe", "matmul", [evR[i - 2] if i >= 2 else None], banks[ybk][:, c0:c0 + N],
                           lhsT=negones[0:nk, :], rhs=SPb[i % 3][0:nk, 0:N], start=True, stop=True)
                if u_["first"]:
                    rz_ev[u_["chunk"]] = I("dve", "memset", [], rb[:, :], 0.0)
                evARG[i] = I("dve", "tensor_tensor",
                             [evY[i], evA[i - 3] if i >= 3 else None, rz_ev[u_["chunk"]],
                              evR[i - 1] if i >= 1 else None],
                             out=ARGb[i % 3][0:nk, 0:N], in0=banks[sbk][0:nk, c0:c0 + N], in1=rb[0:nk, c0:c0 + N],
                             op=ALU.add)
                evR[i] = I("dve", "tensor_tensor", [evY[i]], out=rb[:, c0:c0 + N], in0=banks[ybk][:, c0:c0 + N],
                           in1=rb[:, c0:c0 + N], op=ALU.add)

            def stage2b(i):
                u_ = units[i]
                nk, c0, N = u_["nk"], u_["c0"], u_["N"]
                wts = [evARG[i], evAV[i - 3] if i >= 3 else None]
                if u_["vis"] is None:
                    ev_a = I("act", "activation", wts, out=Ab[i % 3][0:nk, 0:N], in_=ARGb[i % 3][0:nk, 0:N],
                             func=AF.Exp)
                else:
                    ev_a = I("act", "activation", wts, out=Ab[i % 3][0:nk, 0:N], in_=ARGb[i % 3][0:nk, 0:N],
                             func=AF.Exp, bias=flags[0:nk, 2 + u_["vis"]:3 + u_["vis"]])
                if u_["diag"]:
                    dn = min(128, N)
                    ev_a = I("dve", "tensor_tensor", [ev_a], out=Ab[i % 3][0:nk, 0:dn], in0=Ab[i % 3][0:nk, 0:dn],
                             in1=diagm[0:nk, 0:dn], op=ALU.mult)
                evA[i] = ev_a

            def stage3(i):
                u_ = units[i]
                nk, c0, N = u_["nk"], u_["c0"], u_["N"]
                ob = OB_[u_["chunk"] % 2]
                evAV[i] = I("pe", "matmul", [evA[i], o_free[u_["chunk"] % 2] if u_["first"] else None],
                            banks[ob][:, c0:c0 + N], lhsT=u_["v"], rhs=Ab[i % 3][0:nk, 0:N], start=u_["first"],
                            stop=u_["last"], skip_group_check=True)
                if u_["last"]:
                    qn, q0, hh, h = u_["qn"], u_["q0"], u_["hh"], u_["h"]
                    o_free[u_["chunk"] % 2] = I("dve", "tensor_tensor", [evAV[i], sg_ev], out=o_attn[:, h, q0:q0 + qn],
                                                in0=banks[ob][:, 0:qn], in1=SGc[:, hh, q0:q0 + qn], op=ALU.mult)

            nF = len(filler) if filler else 0
            fpos = [int((k + 0.5) * nU / nF) for k in range(nF)]
            for it in range(nU + 4):
                if 0 <= it - 4 < nU:
                    stage3(it - 4)
                if 0 <= it - 3 < nU:
                    stage2b(it - 3)
                if 0 <= it - 2 < nU:
                    stage2(it - 2)
                if 0 <= it - 1 < nU:
                    stage1(it - 1)
                if it < nU:
                    stage0(it)
                while filler and fpos and fpos[0] <= it:
                    fpos.pop(0)
                    filler.pop(0)()
            while filler:
                filler.pop(0)()

        for l in range(depth):
            emit_layer(l)

        yfree = [None, None]
        norm_done = {}
        fsrc = xT0 if DEBUG_STOP else xres
        for ci, (t0, t1) in enumerate(NCH[1:]):
            n = t1 - t0
            par = ci % 2
            ld = P.dma("sp", xst[par][:, :, 0:n], fsrc[:, :, t0:t1].rearrange("c p t -> p c t"), s_x[par],
                       waits=[norm_done.get(ci - 2)])
            r2 = rmsnorm_stats(xst[par], n, par, ld)
            for c in range(KC):
                yb = c % 2
                y_ev = I("dve", "scalar_tensor_tensor", [r2, ld, yfree[yb]], out=yst[:, yb, 0:n],
                         in0=xst[par][:, c, 0:n], scalar=gains[:, NL * 16 + c:NL * 16 + c + 1],
                         in1=rstd[par][:, 0:n], op0=ALU.mult, op1=ALU.mult)
                norm_done[ci] = y_ev
                yfree[yb] = P.dma("sp", outT[c, :, t0 - 16:t1 - 16], yst[:, yb, 0:n], s_out[yb], waits=[y_ev])
        fin = [(so.h, so.cnt) for so in s_out]

        with nc.Block() as block:
            @block.tensor
            def _(e):
                P.replay("pe", e)

            @block.scalar
            def _(e):
                P.replay("act", e)

            @block.vector
            def _(e):
                P.replay("dve", e)

            @block.gpsimd
            def _(e):
                P.replay("pool", e)

            @block.sync
            def _(e):
                P.replay("sp", e)
                for fh, fv in fin:
                    e.wait_ge(fh, fv)
    return nc


def _prep_inputs(x, meta_tokens, norm_gain, w_in, pool_w, pool_scale, w_attn_up, w_pool_up, w_out, final_gain, depth=NL):
    f32 = np.float32
    x = np.asarray(x, f32)
    B = x.shape[0]

    def regroup(w, kcn, ngrp):
        w = np.asarray(w, f32).reshape(NL, kcn, 128, ngrp, GW)
        return np.ascontiguousarray(w.transpose(0, 3, 2, 1, 4)).reshape(NL * ngrp, 128, kcn * GW)

    wir = regroup(w_in, 16, 40).reshape(NL, 40, 128, 16 * GW)
    shared = {f"w_in_r{l}": wir[l] for l in range(depth)}
    shared.update({
        "w_au_r": regroup(w_attn_up, 8, 8)[:depth * 8],
        "w_pu_r": regroup(w_pool_up, 8, 8)[:depth * 8],
        "w_out_r": regroup(w_out, 16, 8)[:depth * 8],
    })
    pw = np.asarray(pool_w, f32).reshape(NL, 4, 2, 128, 256)
    shared["pool_w_r"] = np.ascontiguousarray(pw.transpose(0, 3, 1, 2, 4)).reshape(NL, 128, 4 * 2 * 256)[:depth]
    g = np.concatenate([np.asarray(norm_gain, f32).reshape(NL, 16, 128), np.asarray(final_gain, f32).reshape(1, 16, 128)], 0)
    shared["gains"] = np.ascontiguousarray(g.transpose(2, 0, 1)).reshape(128, NL * 16 + 16)
    ps = np.asarray(pool_scale, f32).reshape(NL, 8, 128)
    shared["pscale"] = np.ascontiguousarray(ps.transpose(2, 0, 1)).reshape(128, NL * 8)
    shared["ones_f32"] = np.ones((128, 128), f32)
    j = np.arange(128)
    shared["negtri"] = -(j[:, None] >= j[None, :]).astype(f32)
    shared["negones"] = -np.ones((128, 128), f32)
    shared["diagmask"] = (j[:, None] < j[None, :]).astype(f32)

    if DEBUG_TINY:
        for k in list(shared):
            if k.startswith("w_"):
                shared[k] = np.ascontiguousarray(shared[k][:1])
    in_maps = []
    for r in range(N_CORES):
        b, half = r // 2, r % 2
        tok = np.zeros((T, D), f32)
        if half == 0:
            tok[0:PRE] = np.asarray(meta_tokens, f32)
        tok[PRE:] = x[b, half * RT:(half + 1) * RT]
        xT0 = np.ascontiguousarray(tok.T).reshape(16, 128, T)
        fl = np.zeros((128, 8), f32)
        if half == 0:
            fl[:, 0] = 1.0; fl[:, 1] = 0.0; fl[:, 2] = 0.0; fl[:, 3] = NEG; fl[:, 4] = 1.0; fl[:, 5] = 0.0
        else:
            fl[:, 0] = 0.0; fl[:, 1] = 1.0; fl[:, 2] = NEG; fl[:, 3] = 0.0; fl[:, 4] = 0.0; fl[:, 5] = 1.0
        fl[:, 6] = 1.0
        fl[:, 7] = EPS
        ic = np.zeros((128, 4, 16), f32)
        for gi, w in enumerate(POOL_W):
            if half == 0:
                ic[:, gi, :] = 1.0 / np.minimum(np.arange(16) + 1, w)
            else:
                ic[:, gi, :] = 1.0 / w
        m = dict(shared)
        m["xT0"] = xT0
        m["flags"] = fl
        m["invcnt"] = ic.reshape(128, 64)
        in_maps.append(m)
    return in_maps


_NC_CACHE = {}


def run(inputs, depth=NL):
    if depth not in _NC_CACHE:
        _NC_CACHE[depth] = build(depth)
    nc = _NC_CACHE[depth]
    in_maps = _prep_inputs(depth=depth, **inputs)
    res = run_bass_kernel_spmd(nc, in_maps, core_ids=list(range(N_CORES)))
    B = inputs["x"].shape[0]
    out = np.zeros((B, 2 * RT, D), np.float32)
    for r in range(N_CORES):
        b, half = r // 2, r % 2
        o = np.asarray(res.results[r]["outT"]).reshape(D, RT)
        out[b, half * RT:(half + 1) * RT, :] = o.T
    return out


def kernel(x, meta_tokens, norm_gain, w_in, pool_w, pool_scale, w_attn_up, w_pool_up, w_out, final_gain):
    return run(dict(x=x, meta_tokens=meta_tokens, norm_gain=norm_gain, w_in=w_in, pool_w=pool_w,
                    pool_scale=pool_scale, w_attn_up=w_attn_up, w_pool_up=w_pool_up, w_out=w_out,
                    final_gain=final_gain))
```

```python
import numpy as np
from contextlib import ExitStack
import concourse.bass as bass
import concourse.mybir as mybir
from concourse.bass_utils import run_bass_kernel_spmd

F32 = mybir.dt.float32
BF16 = mybir.dt.bfloat16
F32R = mybir.dt.float32r
AF = mybir.ActivationFunctionType
ALU = mybir.AluOpType

D = 2048
KC = 16
NL = 4
H = 8
DH = 128
PRE = 16
RT = 1024
T = PRE + RT
NB = RT // 128
GW = 256
PCH = [(0, 347), (347, 694), (694, 1040)]
NCH = [(0, 16)] + [(16 + 256 * i, 16 + 256 * (i + 1)) for i in range(4)]
POOL_W = (2, 4, 8, 16)
EPS = 1e-6
NEG = -30000.0
N_CORES = 8
NWBUF = 3
DEBUG_STOP = None
DEBUG_TINY = False


class DSem:
    def __init__(self, h):
        self.h = h
        self.cnt = 0


class Prog:
    ENG = ("pe", "act", "dve", "pool", "sp")

    def __init__(self, nc, st):
        self.nc = nc
        self.st = st
        self.q = {k: [] for k in self.ENG}
        self.sem = {k: st.enter_context(nc.semaphore("sem_" + k)) for k in self.ENG}
        self.cnt = {k: 0 for k in self.ENG}

    def dsem(self, name):
        return DSem(self.st.enter_context(self.nc.semaphore(name)))

    def I(self, eng, name, waits, *args, sig=True, **kw):
        ev = None
        if sig:
            self.cnt[eng] += 1
            ev = (self.sem[eng], self.cnt[eng])
        self.q[eng].append((name, args, kw, [w for w in waits if w is not None], ev, 1))
        return ev

    def dma(self, eng, out, in_, sem, waits=(), **kw):
        sem.cnt += 16
        ev = (sem.h, sem.cnt)
        kw = dict(kw)
        kw.update(out=out, in_=in_)
        self.q[eng].append(("dma_start", (), kw, [w for w in waits if w is not None], ev, 16))
        return ev

    def cc(self, sem, waits, **kw):
        sem.cnt += 1
        ev = (sem.h, sem.cnt)
        self.q["pool"].append(("collective_compute", (), kw, [w for w in waits if w is not None], ev, 1))
        return ev

    def last(self, eng):
        return (self.sem[eng], self.cnt[eng]) if self.cnt[eng] > 0 else None

    def barrier(self, engs=("pe", "act", "dve", "sp"), extra=()):
        evs = [self.last(o) for o in engs] + list(extra)
        for e in engs:
            self.q[e].append((None, (), {}, [w for w in evs if w is not None and w[0] is not self.sem[e]], None, 0))

    def replay(self, eng, e):
        waited = {}
        for name, args, kw, waits, ev, inc in self.q[eng]:
            for (h, v) in waits:
                k = h.num
                if waited.get(k, 0) >= v:
                    continue
                e.wait_ge(h, v)
                waited[k] = v
            if name is None:
                continue
            ins = getattr(e, name)(*args, **kw)
            if ev is not None:
                ins.then_inc(ev[0], inc)


def build(depth=NL):
    nc = bass.Bass("TRN2", target_bir_lowering=False)

    def din(name, shape, dt=F32):
        return nc.dram_tensor(name, shape, dt, kind="ExternalInput")

    xT0 = din("xT0", [16, 128, T])
    NG = 1 if DEBUG_TINY else 40
    w_in_l = [din(f"w_in_r{l}", [NG, 128, 16 * GW]) for l in range(depth)]
    NG8 = 1 if DEBUG_TINY else depth * 8
    w_au_r = din("w_au_r", [NG8, 128, 8 * GW])
    w_pu_r = din("w_pu_r", [NG8, 128, 8 * GW])
    w_out_r = din("w_out_r", [NG8, 128, 16 * GW])
    pool_w_r = din("pool_w_r", [depth, 128, 4 * 2 * 256])
    gains_d = din("gains", [128, NL * 16 + 16])
    pscale_d = din("pscale", [128, NL * 8])
    flags_d = din("flags", [128, 8])
    invcnt_d = din("invcnt", [128, 4 * 16])
    ones_d = din("ones_f32", [128, 128])
    negtri_d = din("negtri", [128, 128])
    negones_d = din("negones", [128, 128])
    diag_d = din("diagmask", [128, 128])
    outT = nc.dram_tensor("outT", [16, 128, RT], F32, kind="ExternalOutput")

    xres = nc.dram_tensor("xres", [16, 128, T], F32)
    sendK = [[nc.dram_tensor(f"sendK{l}_{a}", [4 * 128, T], BF16) for a in range(2)] for l in range(depth)]
    recvK = [[nc.dram_tensor(f"recvK{l}_{a}", [2 * 4 * 128, T], BF16) for a in range(2)] for l in range(depth)]
    sendV = [[nc.dram_tensor(f"sendV{l}_{a}", [9 * 128, 512], BF16) for a in range(2)] for l in range(depth)]
    recvV = [[nc.dram_tensor(f"recvV{l}_{a}", [2 * 9 * 128, 512], BF16) for a in range(2)] for l in range(depth)]
    sendU = [nc.dram_tensor(f"sendU{l}", [128, 128], F32) for l in range(depth)]
    recvU = [nc.dram_tensor(f"recvU{l}", [256, 128], F32) for l in range(depth)]
    gate_d = [[nc.dram_tensor(f"gate{l}_{k}", [16, 128, T], F32) for k in range(2)] for l in range(depth)]
    GROUPS = [[0, 1], [2, 3], [4, 5], [6, 7]]

    with ExitStack() as st:
        def sb(name, shape, dt):
            return st.enter_context(nc.sbuf_tensor(name, shape, dt))

        P = Prog(nc, st)
        I = P.I

        hT = sb("hT", [128, KC, T], BF16)
        RA = sb("RA", [128, 8768], F32)
        xst_pre = sb("xst_pre", [128, KC, 16], F32)
        RB = sb("RB", [128, 12928], F32)
        o_pool = sb("o_pool", [128, 8, T], BF16)
        wbufs = [sb(f"wbuf{i}", [128, 16 * GW], BF16) for i in range(NWBUF)]
        pool_w_sb = sb("pool_w_sb", [128, 4 * 2 * 256], BF16)
        RC = sb("RC", [128, 6240], F32)
        RD = sb("RD", [128, 4384], F32)
        gains = sb("gains_sb", [128, NL * 16 + 16], F32)
        pscale = sb("pscale_sb", [128, NL * 8], F32)
        flags = sb("flags_sb", [128, 8], F32)
        invcnt = sb("invcnt_sb", [128, 4, 16], F32)
        ones_f = sb("ones_sb", [128, 128], F32)
        negtri = sb("negtri_sb", [128, 128], BF16)
        negones = sb("negones_sb", [128, 128], BF16)
        diagm = sb("diag_sb", [128, 128], BF16)

        def view(region, boff, dt, shape):
            esz = 4 if dt == F32 else 2
            n = int(np.prod(shape[1:])) * esz
            ap = region[:, boff // 4:(boff + n) // 4]
            if dt != F32:
                ap = ap.bitcast(dt)
            if len(shape) == 3:
                ap = ap.rearrange("p (a b) -> p a b", a=shape[1])
            return ap

        xst = [view(RA, i * 16384, F32, [128, KC, 256]) for i in range(2)]
        KT_own = view(RA, 0, BF16, [128, H, T])
        V_own = view(RA, 16640, BF16, [128, 9, 1024])
        merged = view(RA, 0, BF16, [128, KC, T])
        UBUF = view(RB, 0, F32, [128, 8, 16 + T])
        o_attn = view(RB, 0, BF16, [128, H, T])
        ASC = 16640
        Ebuf = view(RB, ASC, F32, [128, 512])
        ARGb = [view(RB, ASC + 2048 + i * 2048, F32, [128, 512]) for i in range(3)]
        Rb = [view(RB, ASC + 8192 + i * 2048, F32, [128, 512]) for i in range(2)]
        SPb = [view(RB, ASC + 12288 + i * 1024, BF16, [128, 512]) for i in range(3)]
        Ab = [view(RB, ASC + 15360 + i * 1024, BF16, [128, 512]) for i in range(3)]
        QT2 = [view(RB, ASC + 18432 + i * 8320, BF16, [128, 2, T]) for i in range(2)]
        SGA2 = [view(RB, ASC + 18432 + 4160 + i * 8320, BF16, [128, 2, T]) for i in range(2)]
        PA = view(RC, 0, F32, [128, 16 + T])
        PBb = view(RC, 4224, F32, [128, 16 + T])
        pooledbf = view(RC, 8448, BF16, [128, 2, T])
        sgp2 = [view(RC, 12608, BF16, [128, 2, T]), view(RC, 17344, BF16, [128, 2, T])]
        uhalo = view(RC, 16768, F32, [128, 8, 16])
        tmp16 = view(RC, 17280, F32, [128, 16])
        SGA = [view(RC, i * 4160, F32, [128, T]) for i in range(2)]
        SGP = [view(RC, 8320 + i * 4160, F32, [128, T]) for i in range(2)]
        T1 = [view(RC, 16640 + i * 4160, F32, [128, T]) for i in range(2)]
        GST = [view(RC, i * 4160, F32, [128, T]) for i in range(2)]
        yst = view(RC, 0, F32, [128, 2, 512])
        KT_par = view(RD, 0, BF16, [128, 4, T])
        V_par = view(RD, 8320, BF16, [128, 4, 9 * 128])
        xc = view(RD, 0, F32, [128, 2, T])
        sq = view(RD, 0, F32, [128, 2, 512])
        sqb = view(RD, 0, BF16, [128, 2, 512])
        rs_tmp = [view(RD, 4096 + i * 1024, F32, [128, 256]) for i in range(2)]
        rstd = [view(RD, 6144 + i * 1024, F32, [128, 256]) for i in range(2)]

        ps_all = st.enter_context(nc.psum_tensor("ps_all", [128, 4096], F32))
        banks = [ps_all[:, 512 * i:512 * (i + 1)] for i in range(8)]
        bank_free = [None] * 8

        s_const = P.dsem("s_const")
        s_w = [P.dsem(f"s_w{i}") for i in range(NWBUF)]
        s_pw = P.dsem("s_pw")
        s_x = [P.dsem(f"s_x{i}") for i in range(3)]
        s_xc = [P.dsem(f"s_xc{i}") for i in range(2)]
        s_xs = [P.dsem(f"s_xs{i}") for i in range(2)]
        s_send = [P.dsem(f"s_send{i}") for i in range(5)]
        s_cck = [P.dsem(f"s_cck{i}") for i in range(2)]
        s_ccv = [P.dsem(f"s_ccv{i}") for i in range(2)]
        s_par = [P.dsem(f"s_par{i}") for i in range(4)]
        s_gst = [P.dsem(f"s_gst{i}") for i in range(2)]
        s_gld = [P.dsem(f"s_gld{i}") for i in range(4)]
        s_hal = P.dsem("s_hal")
        s_out = [P.dsem(f"s_out{i}") for i in range(2)]
        s_cc = [P.dsem(f"s_cc{i}") for i in range(3)]

        for dst, srcd in ((gains, gains_d), (pscale, pscale_d), (flags, flags_d), (ones_f, ones_d)):
            P.dma("pool", dst[:, :], srcd[:, :], s_const)
        P.dma("pool", invcnt[:, :, :], invcnt_d.ap().rearrange("p (a b) -> p a b", a=4), s_const)
        for dst, srcd in ((negtri, negtri_d), (negones, negones_d), (diagm, diag_d)):
            P.dma("pool", dst[:, :], srcd[:, :], s_const)
        const_ev = (s_const.h, s_const.cnt)
        eps_col = flags[:, 7:8]
        I("dve", "memset", [], RC[:, :], 0.0)

        wlist = []
        for l in range(depth):
            def wi(g, l=l):
                return (("in", l, g), w_in_l[l][0 if DEBUG_TINY else g], 16 * GW)
            seq = []
            seq += [wi(16 + g) for g in range(4)]
            seq += [wi(4 + g) for g in range(4)]
            for g in range(4):
                seq += [wi(20 + g), wi(8 + g)]
            seq += [wi(0), wi(12)]
            for hp in range(4):
                if hp < 3:
                    seq += [wi(hp + 1), wi(12 + hp + 1)]
                for mg in (2 * hp, 2 * hp + 1):
                    seq += [wi(24 + mg), wi(32 + mg)]
            for mg in range(8):
                seq += [(("au", l, mg), w_au_r[0 if DEBUG_TINY else l * 8 + mg], 8 * GW),
                        (("pu", l, mg), w_pu_r[0 if DEBUG_TINY else l * 8 + mg], 8 * GW)]
            for og in range(8):
                seq += [(("out", l, og), w_out_r[0 if DEBUG_TINY else l * 8 + og], 16 * GW)]
            wlist += seq
        wstate = {"issued": 0, "next": 0, "free": {}, "ev": {}}

        def w_try_issue():
            while wstate["issued"] < len(wlist):
                i = wstate["issued"]
                if i >= NWBUF and (i - NWBUF) not in wstate["free"]:
                    break
                b = i % NWBUF
                _, srcap, n = wlist[i]
                fr = wstate["free"].get(i - NWBUF)
                wstate["ev"][i] = P.dma("pool", wbufs[b][:, 0:n], srcap, s_w[b], waits=[fr],
                                         max_dma_last_dim=8192)
                wstate["issued"] += 1

        def w_next(kcn, key=None):
            i = wstate["next"]
            wstate["next"] += 1
            if key is not None:
                assert wlist[i][0] == key, (wlist[i][0], key)
            w_try_issue()
            assert i in wstate["ev"], "weight load not issued (too many live groups)"
            b = i % NWBUF
            W = wbufs[b][:, 0:kcn * GW].rearrange("p (k j) -> p k j", k=kcn)
            return W, wstate["ev"][i], i

        def w_release(i, ev):
            wstate["free"][i] = ev
            w_try_issue()

        pbank_rr = {"i": 0}

        def next_bank(lo, hi):
            b = lo + pbank_rr["i"] % (hi - lo)
            pbank_rr["i"] += 1
            return b

        def proj_fm(W, wev, blk, kcn, rhs_fn, evac_fn, extra_waits=(), chunk_waits=None):
            pe_ev = None
            for ci, (t0, t1) in enumerate(PCH):
                n = t1 - t0
                b = next_bank(0, 6)
                for kc in range(kcn):
                    waits = []
                    if kc == 0:
                        waits = [wev, bank_free[b]] + list(extra_waits)
                        if chunk_waits:
                            waits += list(chunk_waits[ci])
                    pe_ev = I("pe", "matmul", waits, banks[b][:, 0:n], lhsT=W[:, kc, blk * 128:(blk + 1) * 128],
                              rhs=rhs_fn(kc, t0, t1), start=(kc == 0), stop=(kc == kcn - 1), sig=(kc == kcn - 1))
                bank_free[b] = evac_fn(banks[b][:, 0:n], ci, t0, t1, pe_ev)
            return pe_ev

        def hT_rhs(kc, t0, t1):
            return hT[:, kc, t0:t1]

        def evac_copy(dst_fn, scale=None, extra=(), eng=None):
            def f(ps, ci, t0, t1, pe_ev):
                w = [pe_ev] + list(extra)
                if (ci % 2 == 0 and eng is None) or eng == "dve":
                    if scale is None:
                        return I("dve", "tensor_copy", w, out=dst_fn(t0, t1), in_=ps)
                    return I("dve", "tensor_scalar", w, out=dst_fn(t0, t1), in0=ps, scalar1=scale,
                             scalar2=None, op0=ALU.mult)
                return I("act", "activation", w, out=dst_fn(t0, t1), in_=ps, func=AF.Copy,
                         scale=(1.0 if scale is None else scale))
            return f

        def evac_act(dst_fn, func, extra_fn=None):
            def f(ps, ci, t0, t1, pe_ev):
                w = [pe_ev] + (list(extra_fn()) if extra_fn else [])
                return I("act", "activation", w, out=dst_fn(t0, t1), in_=ps, func=func)
            return f

        def rmsnorm_stats(stg, n, par, ld):
            b = 6 + par
            sq_free = [None, None]
            pe_ev = None
            for c in range(KC):
                a_ev = I("act", "activation", [ld, sq_free[c % 2], const_ev], out=sqb[:, c % 2, 0:n],
                         in_=stg[:, c, 0:n], func=AF.Square)
                pe_ev = I("pe", "matmul", [a_ev, bank_free[b] if c == 0 else None], banks[b][:, 0:n],
                          lhsT=negones[:, :], rhs=sqb[:, c % 2, 0:n],
                          start=(c == 0), stop=(c == KC - 1))
                sq_free[c % 2] = pe_ev
            r1 = I("act", "activation", [pe_ev], out=rs_tmp[par][:, 0:n], in_=banks[b][:, 0:n], func=AF.Sqrt,
                   bias=eps_col, scale=-1.0 / D)
            r2 = I("dve", "reciprocal", [r1], out=rstd[par][:, 0:n], in_=rs_tmp[par][:, 0:n])
            bank_free[b] = r1
            return r2

        def emit_layer(l):
            src = xT0 if l == 0 else xres
            P.barrier()
            pw_ev = P.dma("pool", pool_w_sb[:, :], pool_w_r[l], s_pw, waits=[P.last("pe")],
                          max_dma_last_dim=8192)

            norm_done = {}
            for ci, (t0, t1) in enumerate(NCH):
                n = t1 - t0
                par = ci % 2
                stg = xst_pre if ci == 0 else xst[par]
                ld = P.dma("sp", stg[:, :, 0:n], src[:, :, t0:t1].rearrange("c p t -> p c t"),
                           s_x[2 if ci == 0 else par], waits=[norm_done.get(ci - 2)])
                r2 = rmsnorm_stats(stg, n, par, ld)
                for c in range(KC):
                    norm_done[ci] = I("dve", "scalar_tensor_tensor", [r2, ld], out=hT[:, c, t0:t1],
                                      in0=stg[:, c, 0:n], scalar=gains[:, l * 16 + c:l * 16 + c + 1],
                                      in1=rstd[par][:, 0:n], op0=ALU.mult, op1=ALU.mult)
            def _cov(t0, t1):
                return [norm_done[ci] for ci, (a, b) in enumerate(NCH) if a < t1 and b > t0]
            hT_chunk_ready = [_cov(t0, t1) for (t0, t1) in PCH]
            normA_done = norm_done[len(NCH) - 1]
            if DEBUG_STOP == "A":
                P.barrier()
                return

            z_ev = I("dve", "memset", [], UBUF[:, :, 0:16], 0.0)
            vz_ev = I("dve", "memset", [normA_done], V_own[:, 0, :], 0.0)
            for g in range(4):
                W, wev, wi_ = w_next(16)
                for blk in range(2):
                    ub = 2 * g + blk
                    pe_ev = proj_fm(W, wev, blk, 16, hT_rhs,
                                    evac_copy(lambda t0, t1, ub=ub: UBUF[:, ub, 16 + t0:16 + t1]),
                                    chunk_waits=(hT_chunk_ready if (g == 0 and blk == 0) else None))
                w_release(wi_, pe_ev)
            u_done = [P.last("dve"), P.last("act")]
            if DEBUG_STOP == "Ba":
                P.barrier()
                return
            su = P.dma("sp", sendU[l].ap().rearrange("p (a b) -> p a b", a=8), UBUF[:, :, T:T + 16],
                       s_send[0], waits=u_done)
            cc_u_ev = P.cc(s_cc[0], [su], kind="AllGather", op=ALU.bypass, replica_groups=GROUPS,
                           ins=[sendU[l].ap().opt()], outs=[recvU[l].ap().opt()])
            if DEBUG_STOP == "Bb":
                P.barrier(extra=[cc_u_ev])
                return

            hl = P.dma("sp", uhalo[:, :, :], recvU[l][0:128, :].rearrange("p (a b) -> p a b", a=8), s_hal,
                       waits=[cc_u_ev])
            h1 = I("dve", "tensor_scalar", [hl, const_ev], out=uhalo[:, :, :], in0=uhalo[:, :, :],
                   scalar1=flags[:, 5:6], scalar2=None, op0=ALU.mult)
            h2 = I("dve", "scalar_tensor_tensor", [h1, z_ev] + u_done, out=UBUF[:, :, 16:32], in0=UBUF[:, :, 16:32],
                   scalar=flags[:, 4:5], in1=uhalo[:, :, :], op0=ALU.mult, op1=ALU.add)

            for g in range(4):
                W, wev, wi_ = w_next(16, ("in", l, 4 + g))
                for blk in range(2):
                    h = 2 * g + blk
                    pe_ev = proj_fm(W, wev, blk, 16, hT_rhs,
                                    evac_copy(lambda t0, t1, h=h: KT_own[:, h, t0:t1], eng="act",
                                              extra=[normA_done]))
                w_release(wi_, pe_ev)
            k_done = [P.last("act")]
            cc_k_ev = []
            for a in range(2):
                sk = P.dma("sp", sendK[l][a].ap().rearrange("(h p) t -> p h t", p=128),
                           KT_own[:, 4 * a:4 * a + 4, :], s_send[1 + a], waits=k_done)
                cc_k_ev.append(P.cc(s_cck[a], [sk], kind="AllGather", op=ALU.bypass, replica_groups=GROUPS,
                                    ins=[sendK[l][a].ap().opt()], outs=[recvK[l][a].ap().opt()]))

            tblocks = [(0, 16)] + [(16 + 128 * i, 16 + 128 * (i + 1)) for i in range(NB)]
            opool_done = {}
            cc_v_ev = []
            for pg in range(4):
                w = POOL_W[pg]
                sgp = sgp2[pg % 2]
                for j in range(2):
                    ub = 2 * pg + j
                    u = UBUF[:, ub, :]
                    cur = u
                    d = 1
                    tgl = 0
                    ev = h2
                    while d < w:
                        dst = PA if tgl == 0 else PBb
                        ev = I("dve", "tensor_tensor", [ev], out=dst[:, d:16 + T], in0=cur[:, d:16 + T],
                               in1=cur[:, 0:16 + T - d], op=ALU.add)
                        cur = dst
                        d *= 2
                        tgl ^= 1
                    I("dve", "scalar_tensor_tensor", [ev], out=pooledbf[:, j, 16:T], in0=cur[:, 32:16 + T],
                      scalar=1.0 / w, in1=u[:, 32:16 + T], op0=ALU.mult, op1=ALU.subtract)
                    t16 = I("dve", "tensor_tensor", [ev, const_ev], out=tmp16[:, :], in0=cur[:, 16:32],
                            in1=invcnt[:, pg, :], op=ALU.mult)
                    I("dve", "tensor_tensor", [t16], out=pooledbf[:, j, 0:16], in0=tmp16[:, :], in1=u[:, 16:32],
                      op=ALU.subtract)
                pl_ev = P.last("dve")
                W, wev, wi_ = w_next(16, ("in", l, 20 + pg))
                for blk in range(2):
                    pe_ev = proj_fm(W, wev, blk, 16, hT_rhs,
                                    evac_act(lambda t0, t1, blk=blk, sgp=sgp: sgp[:, blk, t0:t1], AF.Silu,
                                             extra_fn=lambda pg=pg: [opool_done.get(pg - 2)]))
                w_release(wi_, pe_ev)
                sgp_ev = P.last("act")
                W, wev, wi_ = w_next(16, ("in", l, 8 + pg))
                pe_ev = None
                for tb, (t0, t1) in enumerate(tblocks):
                    m = t1 - t0
                    b = next_bank(0, 6)
                    for kc in range(KC):
                        pe_ev = I("pe", "matmul", ([wev, bank_free[b]] if kc == 0 else []), banks[b][0:m, 0:GW],
                                  lhsT=hT[:, kc, t0:t1], rhs=W[:, kc, :], start=(kc == 0), stop=(kc == KC - 1),
                                  sig=(kc == KC - 1))
                    bank_free[b] = I("act", "activation", [pe_ev, vz_ev if tb == 0 else None],
                                     out=V_own[0:m, tb, pg * GW:(pg + 1) * GW], in_=banks[b][0:m, 0:GW], func=AF.Copy)
                w_release(wi_, pe_ev)
                if pg % 2 == 1:
                    a = pg // 2
                    sv = P.dma("sp", sendV[l][a].ap().rearrange("(b p) c -> p b c", p=128),
                               V_own[:, :, 512 * a:512 * (a + 1)], s_send[3 + a], waits=[P.last("act")])
                    cc_v_ev.append(P.cc(s_ccv[a], [sv], kind="AllGather", op=ALU.bypass, replica_groups=GROUPS,
                                        ins=[sendV[l][a].ap().opt()], outs=[recvV[l][a].ap().opt()]))
                for ob in range(2):
                    for ci, (t0, t1) in enumerate(PCH):
                        n = t1 - t0
                        b = next_bank(0, 6)
                        for kc in range(2):
                            o0 = (pg * 2 + kc) * 256 + ob * 128
                            pe_ev = I("pe", "matmul", ([pl_ev, pw_ev, bank_free[b]] if kc == 0 else []),
                                      banks[b][:, 0:n], lhsT=pool_w_sb[:, o0:o0 + 128], rhs=pooledbf[:, kc, t0:t1],
                                      start=(kc == 0), stop=(kc == 1), sig=(kc == 1))
                        cidx = l * 8 + 2 * pg + ob
                        bank_free[b] = I("dve", "scalar_tensor_tensor", [pe_ev, sgp_ev, const_ev],
                                         out=o_pool[:, 2 * pg + ob, t0:t1], in0=banks[b][:, 0:n],
                                         scalar=pscale[:, cidx:cidx + 1], in1=sgp[:, ob, t0:t1],
                                         op0=ALU.mult, op1=ALU.mult)
                opool_done[pg] = P.last("dve")
            P.barrier()
            if DEBUG_STOP == "B":
                return

            def qga_tasks(hp, bank_rng):
                tasks = []
                stt = {}

                def mk(kind, blk, ci):
                    def run():
                        if (kind, "W") not in stt:
                            stt[(kind, "W")] = w_next(16, ("in", l, (hp if kind == "q" else 12 + hp)))
                        W, wev, wi_ = stt[(kind, "W")]
                        t0, t1 = PCH[ci]
                        n = t1 - t0
                        b = next_bank(*bank_rng)
                        pe_ev = None
                        for kc in range(KC):
                            pe_ev = I("pe", "matmul", ([wev, bank_free[b]] if kc == 0 else []), banks[b][:, 0:n],
                                      lhsT=W[:, kc, blk * 128:(blk + 1) * 128], rhs=hT[:, kc, t0:t1],
                                      start=(kc == 0), stop=(kc == KC - 1), sig=(kc == KC - 1))
                        if kind == "q":
                            bank_free[b] = I("dve", "tensor_scalar", [pe_ev], out=QT2[hp % 2][:, blk, t0:t1],
                                             in0=banks[b][:, 0:n], scalar1=float(DH) ** -0.5, scalar2=None,
                                             op0=ALU.mult)
                        else:
                            bank_free[b] = I("dve", "tensor_copy", [pe_ev], out=SGA2[hp % 2][:, blk, t0:t1],
                                             in_=banks[b][:, 0:n])
                        if blk == 1 and ci == len(PCH) - 1:
                            w_release(wi_, pe_ev)
                    return run
                for kind in ("q", "g"):
                    for blk in range(2):
                        for ci in range(len(PCH)):
                            tasks.append(mk(kind, blk, ci))
                return tasks

            gst_state = {"free": [None, None], "n": 0, "store": {}}

            def gate_tasks(hp):
                tasks = []
                stt = {}

                def mk(kind, mg, blk, ci):
                    def run():
                        wk = (kind, mg)
                        if wk not in stt:
                            stt[wk] = w_next(16, ("in", l, (24 if kind == 0 else 32) + mg))
                        W, wev, wi_ = stt[wk]
                        t0, t1 = PCH[ci]
                        n = t1 - t0
                        if ci == 0:
                            stt["sidx"] = gst_state["n"] % 2
                            gst_state["n"] += 1
                            stt["evs"] = []
                        sidx = stt["sidx"]
                        b = 7
                        pe_ev = None
                        for kc in range(KC):
                            pe_ev = I("pe", "matmul", ([wev, bank_free[b]] if kc == 0 else []), banks[b][:, 0:n],
                                      lhsT=W[:, kc, blk * 128:(blk + 1) * 128], rhs=hT[:, kc, t0:t1],
                                      start=(kc == 0), stop=(kc == KC - 1), sig=(kc == KC - 1))
                        bank_free[b] = I("dve", "tensor_copy", [pe_ev, gst_state["free"][sidx] if ci == 0 else None],
                                         out=GST[sidx][:, t0:t1], in_=banks[b][:, 0:n])
                        stt["evs"].append(bank_free[b])
                        if ci == len(PCH) - 1:
                            c = 2 * mg + blk
                            ev = P.dma("sp", gate_d[l][kind][c], GST[sidx][:, :], s_gst[sidx], waits=stt["evs"])
                            gst_state["free"][sidx] = ev
                            gst_state["store"][(kind, c)] = ev
                            if blk == 1:
                                w_release(wi_, pe_ev)
                    return run
                for mg in (2 * hp, 2 * hp + 1):
                    for kind in (0, 1):
                        for blk in range(2):
                            for ci in range(len(PCH)):
                                tasks.append(mk(kind, mg, blk, ci))
                return tasks

            for t_ in qga_tasks(0, (0, 6)):
                t_()

            def par_loads(hp, free_ev):
                evs = []
                for hh in range(2):
                    h = 2 * hp + hh
                    slot = h % 4
                    ha, hl_ = h // 4, h % 4
                    P.dma("sp", KT_par[:, slot, :], recvK[l][ha][hl_ * 128:(hl_ + 1) * 128, :], s_par[slot],
                          waits=[cc_k_ev[ha], free_ev])
                    e2 = P.dma("sp", V_par[:, slot, :].rearrange("p (b c) -> p b c", b=9),
                               recvV[l][ha][0:9 * 128, hl_ * 128:(hl_ + 1) * 128].rearrange("(b p) c -> p b c", p=128),
                               s_par[slot], waits=[cc_v_ev[ha], free_ev])
                    evs.append(e2)
                return evs

            pair_done = {}
            par_evs = {0: par_loads(0, None)}
            for hp in range(4):
                if hp < 3:
                    par_evs[hp + 1] = par_loads(hp + 1, pair_done.get(hp - 1))
                P.barrier(engs=("pe", "act", "dve"))
                att_ctx["par"] = par_evs[hp]
                attention_pair(hp, filler=((qga_tasks(hp + 1, (7, 8)) if hp < 3 else []) + gate_tasks(hp)))
                pair_done[hp] = P.last("pe")
            P.barrier()
            if DEBUG_STOP == "C":
                return
            def load_gate(kind, mg, blk, free_ev):
                c = 2 * mg + blk
                buf = (SGA if kind == 0 else SGP)[blk]
                ld = P.dma("sp", buf[:, :], gate_d[l][kind][c], s_gld[kind * 2 + blk],
                           waits=[gst_state["store"][(kind, c)], free_ev])
                return I("act", "activation", [ld], out=buf[:, :], in_=buf[:, :], func=AF.Sigmoid)

            ga_ev = {}
            gp_ev = {}
            for blk in range(2):
                ga_ev[(0, blk)] = load_gate(0, 0, blk, gst_state["free"][blk])
                gp_ev[(0, blk)] = load_gate(1, 0, blk, None)
            for mg in range(8):
                W, wev, wi_ = w_next(8, ("au", l, mg))
                for blk in range(2):
                    def ev_ya(ps, ci, t0, t1, pe_ev, blk=blk):
                        return I("dve", "tensor_tensor", [pe_ev, ga_ev[(mg, blk)]], out=T1[blk][:, t0:t1],
                                 in0=ps, in1=SGA[blk][:, t0:t1], op=ALU.mult)
                    pe_ev = proj_fm(W, wev, blk, 8, lambda kc, t0, t1: o_attn[:, kc, t0:t1], ev_ya)
                    if mg < 7:
                        ga_ev[(mg + 1, blk)] = load_gate(0, mg + 1, blk, P.last("dve"))
                w_release(wi_, pe_ev)
                W, wev, wi_ = w_next(8, ("pu", l, mg))
                for blk in range(2):
                    c = 2 * mg + blk

                    def ev_yp(ps, ci, t0, t1, pe_ev, blk=blk, c=c):
                        e1 = I("dve", "tensor_tensor", [pe_ev, gp_ev[(mg, blk)]], out=SGP[blk][:, t0:t1], in0=ps,
                               in1=SGP[blk][:, t0:t1], op=ALU.mult)
                        return I("dve", "tensor_tensor", [e1], out=merged[:, c, t0:t1], in0=T1[blk][:, t0:t1],
                                 in1=SGP[blk][:, t0:t1], op=ALU.add)
                    pe_ev = proj_fm(W, wev, blk, 8, lambda kc, t0, t1: o_pool[:, kc, t0:t1], ev_yp)
                    if mg < 7:
                        gp_ev[(mg + 1, blk)] = load_gate(1, mg + 1, blk, P.last("dve"))
                w_release(wi_, pe_ev)
            P.barrier()
            if DEBUG_STOP == "D":
                return

            xs_ev = [None, None]
            for og in range(8):
                W, wev, wi_ = w_next(16)
                for blk in range(2):
                    c = 2 * og + blk
                    xb = c % 2
                    ld = P.dma("sp", xc[:, xb, :], src[c], s_xc[xb], waits=[xs_ev[xb]])

                    def ev_res(ps, ci, t0, t1, pe_ev, xb=xb, ld=ld):
                        return I("dve", "tensor_tensor", [pe_ev, ld], out=xc[:, xb, t0:t1], in0=ps,
                                 in1=xc[:, xb, t0:t1], op=ALU.add)
                    pe_ev = proj_fm(W, wev, blk, 16, lambda kc, t0, t1: merged[:, kc, t0:t1], ev_res)
                    xs_ev[xb] = P.dma("sp", xres[c], xc[:, xb, :], s_xs[xb], waits=[P.last("dve")])
                w_release(wi_, pe_ev)
            P.barrier(extra=[xs_ev[0], xs_ev[1]])

        att_ctx = {}

        def attention_pair(hp, filler=None):
            QTc = QT2[hp % 2]
            SGc = SGA2[hp % 2]
            sg_ev = I("act", "activation", [], out=SGc[:, :, :], in_=SGc[:, :, :], func=AF.Silu)
            units = []
            chunk_id = 0
            for hh in range(2):
                h = 2 * hp + hh

                def own_blk(kb):
                    return (KT_own[:, h, 16 + 128 * kb:16 + 128 * (kb + 1)],
                            V_own[:, 1 + kb, h * 128:(h + 1) * 128], 128)

                def own_pre():
                    return (KT_own[:, h, 0:16], V_own[0:16, 0, h * 128:(h + 1) * 128], 16)

                slot = h % 4

                def par_blk(kb):
                    return (KT_par[:, slot, 16 + 128 * kb:16 + 128 * (kb + 1)],
                            V_par[:, slot, (1 + kb) * 128:(2 + kb) * 128], 128)

                def par_pre():
                    return (KT_par[:, slot, 0:16], V_par[0:16, slot, 0:128], 16)

                for c in range(2):
                    q0 = 16 + 512 * c
                    ul = []
                    for i in (3, 2, 1, 0):
                        kt, v, nk = own_blk(4 * c + i)
                        ul.append(dict(kt=kt, v=v, nk=nk, c0=128 * i, N=512 - 128 * i, vis=None, diag=True))
                    for kb in range(4 * c - 1, -1, -1):
                        kt, v, nk = own_blk(kb)
                        ul.append(dict(kt=kt, v=v, nk=nk, c0=0, N=512, vis=None, diag=False))
                    kt, v, nk = own_pre()
                    ul.append(dict(kt=kt, v=v, nk=nk, c0=0, N=512, vis=0, diag=False))
                    for kb in range(NB - 1, -1, -1):
                        kt, v, nk = par_blk(kb)
                        ul.append(dict(kt=kt, v=v, nk=nk, c0=0, N=512, vis=1, diag=False, pw=att_ctx["par"][hh]))
                    kt, v, nk = par_pre()
                    ul.append(dict(kt=kt, v=v, nk=nk, c0=0, N=512, vis=1, diag=False, pw=att_ctx["par"][hh]))
                    for k, u_ in enumerate(ul):
                        u_.update(q0=q0, hh=hh, h=h, first=(k == 0), last=(k == len(ul) - 1), chunk=chunk_id, qn=512)
                    units += ul
                    chunk_id += 1
                kt, v, nk = own_pre()
                units.append(dict(kt=kt, v=v, nk=nk, c0=0, N=16, vis=None, diag=True, q0=0, hh=hh, h=h,
                                  first=True, last=True, chunk=chunk_id, qn=16))
                chunk_id += 1

            nU = len(units)
            evE = [None] * nU
            evSP = [None] * nU
            evY = [None] * nU
            evARG = [None] * nU
            evR = [None] * nU
            evA = [None] * nU
            evAV = [None] * nU
            o_free = {0: None, 1: None}
            rz_ev = {}
            SB_ = (0, 1, 2)
            YB_ = (3, 4)
            OB_ = (5, 6)

            def stage0(i):
                u_ = units[i]
                nk, c0, N = u_["nk"], u_["c0"], u_["N"]
                sbk = SB_[i % 3]
                qap = QTc[:, u_["hh"], u_["q0"] + c0:u_["q0"] + c0 + N]
                ev_s = I("pe", "matmul", [evARG[i - 3] if i >= 3 else None, u_.get("pw")], banks[sbk][0:nk, c0:c0 + N],
                         lhsT=u_["kt"], rhs=qap, start=True, stop=False, skip_group_check=True)
                evE[i] = I("act", "activation", [ev_s], out=Ebuf[0:nk, 0:N], in_=banks[sbk][0:nk, c0:c0 + N],
                           func=AF.Exp)

            def stage1(i):
                u_ = units[i]
                nk, c0, N = u_["nk"], u_["c0"], u_["N"]
                sc = 1.0 if u_["vis"] is None else flags[0:nk, u_["vis"]:u_["vis"] + 1]
                ev_sp = I("act", "activation", [evE[i], evY[i - 3] if i >= 3 else None, const_ev],
                          out=SPb[i % 3][0:nk, 0:N], in_=Ebuf[0:nk, 0:N], func=AF.Ln,
                          bias=1.0, scale=sc)
                if u_["diag"]:
                    dn = min(128, N)
                    ev_sp = I("dve", "tensor_tensor", [ev_sp, const_ev], out=SPb[i % 3][0:nk, 0:dn],
                              in0=SPb[i % 3][0:nk, 0:dn], in1=diagm[0:nk, 0:dn], op=ALU.mult)
                evSP[i] = ev_sp

            def stage2(i):
                u_ = units[i]
                nk, c0, N = u_["nk"], u_["c0"], u_["N"]
                sbk = SB_[i % 3]
                ybk = YB_[i % 2]
                rb = Rb[u_["chunk"] % 2]
                I("pe", "matmul", [evSP[i], const_ev], banks[sbk][0:nk, c0:c0 + N], lhsT=negtri[0:nk, 0:nk],
                  rhs=SPb[i % 3][0:nk, 0:N], start=False, stop=True, skip_group_check=True, sig=False)
                evY[i] = I("pe", "matmul", [evR[i - 2] if i >= 2 else None], banks[ybk][:, c0:c0 + N],
                           lhsT=negones[0:nk, :], rhs=SPb[i % 3][0:nk, 0:N], start=True, stop=True)
                if u_["first"]:
                    rz_ev[u_["chunk"]] = I("dve", "memset", [], rb[:, :], 0.0)
                evARG[i] = I("dve", "tensor_tensor",
                             [evY[i], evA[i - 3] if i >= 3 else None, rz_ev[u_["chunk"]],
                              evR[i - 1] if i >= 1 else None],
                             out=ARGb[i % 3][0:nk, 0:N], in0=banks[sbk][0:nk, c0:c0 + N], in1=rb[0:nk, c0:c0 + N],
                             op=ALU.add)
                evR[i] = I("dve", "tensor_tensor", [evY[i]], out=rb[:, c0:c0 + N], in0=banks[ybk][:, c0:c0 + N],
                           in1=rb[:, c0:c0 + N], op=ALU.add)

            def stage2b(i):
                u_ = units[i]
                nk, c0, N = u_["nk"], u_["c0"], u_["N"]
                wts = [evARG[i], evAV[i - 3] if i >= 3 else None]
                if u_["vis"] is None:
                    ev_a = I("act", "activation", wts, out=Ab[i % 3][0:nk, 0:N], in_=ARGb[i % 3][0:nk, 0:N],
                             func=AF.Exp)
                else:
                    ev_a = I("act", "activation", wts, out=Ab[i % 3][0:nk, 0:N], in_=ARGb[i % 3][0:nk, 0:N],
                             func=AF.Exp, bias=flags[0:nk, 2 + u_["vis"]:3 + u_["vis"]])
                if u_["diag"]:
                    dn = min(128, N)
                    ev_a = I("dve", "tensor_tensor", [ev_a], out=Ab[i % 3][0:nk, 0:dn], in0=Ab[i % 3][0:nk, 0:dn],
                             in1=diagm[0:nk, 0:dn], op=ALU.mult)
                evA[i] = ev_a

            def stage3(i):
                u_ = units[i]
                nk, c0, N = u_["nk"], u_["c0"], u_["N"]
                ob = OB_[u_["chunk"] % 2]
                evAV[i] = I("pe", "matmul", [evA[i], o_free[u_["chunk"] % 2] if u_["first"] else None],
                            banks[ob][:, c0:c0 + N], lhsT=u_["v"], rhs=Ab[i % 3][0:nk, 0:N], start=u_["first"],
                            stop=u_["last"], skip_group_check=True)
                if u_["last"]:
                    qn, q0, hh, h = u_["qn"], u_["q0"], u_["hh"], u_["h"]
                    o_free[u_["chunk"] % 2] = I("dve", "tensor_tensor", [evAV[i], sg_ev], out=o_attn[:, h, q0:q0 + qn],
                                                in0=banks[ob][:, 0:qn], in1=SGc[:, hh, q0:q0 + qn], op=ALU.mult)

            nF = len(filler) if filler else 0
            fpos = [int((k + 0.5) * nU / nF) for k in range(nF)]
            for it in range(nU + 4):
                if 0 <= it - 4 < nU:
                    stage3(it - 4)
                if 0 <= it - 3 < nU:
                    stage2b(it - 3)
                if 0 <= it - 2 < nU:
                    stage2(it - 2)
                if 0 <= it - 1 < nU:
                    stage1(it - 1)
                if it < nU:
                    stage0(it)
                while filler and fpos and fpos[0] <= it:
                    fpos.pop(0)
                    filler.pop(0)()
            while filler:
                filler.pop(0)()

        for l in range(depth):
            emit_layer(l)

        yfree = [None, None]
        norm_done = {}
        fsrc = xT0 if DEBUG_STOP else xres
        for ci, (t0, t1) in enumerate(NCH[1:]):
            n = t1 - t0
            par = ci % 2
            ld = P.dma("sp", xst[par][:, :, 0:n], fsrc[:, :, t0:t1].rearrange("c p t -> p c t"), s_x[par],
                       waits=[norm_done.get(ci - 2)])
            r2 = rmsnorm_stats(xst[par], n, par, ld)
            for c in range(KC):
                yb = c % 2
                y_ev = I("dve", "scalar_tensor_tensor", [r2, ld, yfree[yb]], out=yst[:, yb, 0:n],
                         in0=xst[par][:, c, 0:n], scalar=gains[:, NL * 16 + c:NL * 16 + c + 1],
                         in1=rstd[par][:, 0:n], op0=ALU.mult, op1=ALU.mult)
                norm_done[ci] = y_ev
                yfree[yb] = P.dma("sp", outT[c, :, t0 - 16:t1 - 16], yst[:, yb, 0:n], s_out[yb], waits=[y_ev])
        fin = [(so.h, so.cnt) for so in s_out]

        with nc.Block() as block:
            @block.tensor
            def _(e):
                P.replay("pe", e)

            @block.scalar
            def _(e):
                P.replay("act", e)

            @block.vector
            def _(e):
                P.replay("dve", e)

            @block.gpsimd
            def _(e):
                P.replay("pool", e)

            @block.sync
            def _(e):
                P.replay("sp", e)
                for fh, fv in fin:
                    e.wait_ge(fh, fv)
    return nc


def _prep_inputs(x, meta_tokens, norm_gain, w_in, pool_w, pool_scale, w_attn_up, w_pool_up, w_out, final_gain, depth=NL):
    f32 = np.float32
    x = np.asarray(x, f32)
    B = x.shape[0]

    def regroup(w, kcn, ngrp):
        w = np.asarray(w, f32).reshape(NL, kcn, 128, ngrp, GW)
        return np.ascontiguousarray(w.transpose(0, 3, 2, 1, 4)).reshape(NL * ngrp, 128, kcn * GW)

    wir = regroup(w_in, 16, 40).reshape(NL, 40, 128, 16 * GW)
    shared = {f"w_in_r{l}": wir[l] for l in range(depth)}
    shared.update({
        "w_au_r": regroup(w_attn_up, 8, 8)[:depth * 8],
        "w_pu_r": regroup(w_pool_up, 8, 8)[:depth * 8],
        "w_out_r": regroup(w_out, 16, 8)[:depth * 8],
    })
    pw = np.asarray(pool_w, f32).reshape(NL, 4, 2, 128, 256)
    shared["pool_w_r"] = np.ascontiguousarray(pw.transpose(0, 3, 1, 2, 4)).reshape(NL, 128, 4 * 2 * 256)[:depth]
    g = np.concatenate([np.asarray(norm_gain, f32).reshape(NL, 16, 128), np.asarray(final_gain, f32).reshape(1, 16, 128)], 0)
    shared["gains"] = np.ascontiguousarray(g.transpose(2, 0, 1)).reshape(128, NL * 16 + 16)
    ps = np.asarray(pool_scale, f32).reshape(NL, 8, 128)
    shared["pscale"] = np.ascontiguousarray(ps.transpose(2, 0, 1)).reshape(128, NL * 8)
    shared["ones_f32"] = np.ones((128, 128), f32)
    j = np.arange(128)
    shared["negtri"] = -(j[:, None] >= j[None, :]).astype(f32)
    shared["negones"] = -np.ones((128, 128), f32)
    shared["diagmask"] = (j[:, None] < j[None, :]).astype(f32)

    if DEBUG_TINY:
        for k in list(shared):
            if k.startswith("w_"):
                shared[k] = np.ascontiguousarray(shared[k][:1])
    in_maps = []
    for r in range(N_CORES):
        b, half = r // 2, r % 2
        tok = np.zeros((T, D), f32)
        if half == 0:
            tok[0:PRE] = np.asarray(meta_tokens, f32)
        tok[PRE:] = x[b, half * RT:(half + 1) * RT]
        xT0 = np.ascontiguousarray(tok.T).reshape(16, 128, T)
        fl = np.zeros((128, 8), f32)
        if half == 0:
            fl[:, 0] = 1.0; fl[:, 1] = 0.0; fl[:, 2] = 0.0; fl[:, 3] = NEG; fl[:, 4] = 1.0; fl[:, 5] = 0.0
        else:
            fl[:, 0] = 0.0; fl[:, 1] = 1.0; fl[:, 2] = NEG; fl[:, 3] = 0.0; fl[:, 4] = 0.0; fl[:, 5] = 1.0
        fl[:, 6] = 1.0
        fl[:, 7] = EPS
        ic = np.zeros((128, 4, 16), f32)
        for gi, w in enumerate(POOL_W):
            if half == 0:
                ic[:, gi, :] = 1.0 / np.minimum(np.arange(16) + 1, w)
            else:
                ic[:, gi, :] = 1.0 / w
        m = dict(shared)
        m["xT0"] = xT0
        m["flags"] = fl
        m["invcnt"] = ic.reshape(128, 64)
        in_maps.append(m)
    return in_maps


_NC_CACHE = {}


def run(inputs, depth=NL):
    if depth not in _NC_CACHE:
        _NC_CACHE[depth] = build(depth)
    nc = _NC_CACHE[depth]
    in_maps = _prep_inputs(depth=depth, **inputs)
    res = run_bass_kernel_spmd(nc, in_maps, core_ids=list(range(N_CORES)))
    B = inputs["x"].shape[0]
    out = np.zeros((B, 2 * RT, D), np.float32)
    for r in range(N_CORES):
        b, half = r // 2, r % 2
        o = np.asarray(res.results[r]["outT"]).reshape(D, RT)
        out[b, half * RT:(half + 1) * RT, :] = o.T
    return out


def kernel(x, meta_tokens, norm_gain, w_in, pool_w, pool_scale, w_attn_up, w_pool_up, w_out, final_gain):
    return run(dict(x=x, meta_tokens=meta_tokens, norm_gain=norm_gain, w_in=w_in, pool_w=pool_w,
                    pool_scale=pool_scale, w_attn_up=w_attn_up, w_pool_up=w_pool_up, w_out=w_out,
                    final_gain=final_gain))
```
